# Optimizing a Trainium2 kernel written in Bass

```python
import math
import jax, jax.numpy as jnp
from jax import lax
import numpy as np

D_MODEL = 1024
BATCH = 2
SEQ = 8192
DEPTH = 4

CHUNK = 64
N_META = 16
N_A = DEPTH // 2
N_B = DEPTH - N_A
D_FF = 2816
POOL_WINDOWS = (2, 4, 8, 16)
N_POOL_GROUPS = len(POOL_WINDOWS)
POOL_GROUP = D_MODEL // N_POOL_GROUPS
N_HEADS = 8
QK_NOPE = 64
QK_ROPE = 32
V_HEAD = 64
KV_RANK = 256
Q_RANK = 384
ROPE_THETA = 10000.0
Q_BLOCK = 128
EPS = 1e-6

kernel_name = "yoco_pool_mla_macaron_trunk"


def rmsnorm(x, g):
    xf = x.astype(jnp.float32)
    xf = xf * lax.rsqrt(jnp.mean(xf * xf, axis=-1, keepdims=True) + EPS)
    return xf.astype(x.dtype) * g


def swiglu(h, w_gate, w_up, w_down):
    return (jax.nn.silu(h @ w_gate) * (h @ w_up)) @ w_down


def chunk_ids(n):
    pos = jnp.arange(n)
    return jnp.where(pos < N_META, 0, (pos - N_META) // CHUNK + 1)


def rope_tables(n):
    inv = 1.0 / (ROPE_THETA ** (jnp.arange(0, QK_ROPE, 2, dtype=jnp.float32) / QK_ROPE))
    ang = jnp.arange(n, dtype=jnp.float32)[:, None] * inv[None, :]
    return jnp.cos(ang), jnp.sin(ang)


def apply_rope(x, cos, sin):
    xf = x.astype(jnp.float32)
    x1, x2 = xf[..., : QK_ROPE // 2], xf[..., QK_ROPE // 2:]
    out = jnp.concatenate([x1 * cos - x2 * sin, x2 * cos + x1 * sin], axis=-1)
    return out.astype(x.dtype)


def pool_mixer(h, w_group, scale):
    L = h.shape[1]
    hf = h.astype(jnp.float32)
    cs = jnp.concatenate([jnp.zeros_like(hf[:, :1]), jnp.cumsum(hf, axis=1)], axis=1)
    hi = jnp.arange(1, L + 1)
    outs = []
    for g, w in enumerate(POOL_WINDOWS):
        sl = slice(g * POOL_GROUP, (g + 1) * POOL_GROUP)
        lo = jnp.maximum(hi - w, 0)
        c = cs[..., sl]
        count = (hi - lo).astype(jnp.float32)[None, :, None]
        mean = (jnp.take(c, hi, axis=1) - jnp.take(c, lo, axis=1)) / count
        outs.append(mean - hf[..., sl])
    pooled = jnp.stack(outs, axis=2).astype(h.dtype)
    y = jnp.einsum('blgc,gcd->blgd', pooled, w_group)
    return y.reshape(h.shape) * scale


def mla_shared_kv(h, w_dkv, kv_latent_norm, w_uk, w_uv, cos, sin):
    B, L, _ = h.shape
    ckr = h @ w_dkv
    c_kv = rmsnorm(ckr[..., :KV_RANK], kv_latent_norm)
    k_rope = apply_rope(ckr[..., KV_RANK:], cos, sin)
    k_nope = (c_kv @ w_uk).reshape(B, L, N_HEADS, QK_NOPE)
    v = (c_kv @ w_uv).reshape(B, L, N_HEADS, V_HEAD)
    return k_nope, k_rope, v


def mla_attention(h, w_dq, q_latent_norm, w_uq, w_o, k_nope, k_rope, v, cos, sin):
    B, L, _ = h.shape
    cq = rmsnorm(h @ w_dq, q_latent_norm)
    q = (cq @ w_uq).reshape(B, L, N_HEADS, QK_NOPE + QK_ROPE)
    q_nope = q[..., :QK_NOPE]
    q_rope = apply_rope(q[..., QK_NOPE:], cos[:, None, :], sin[:, None, :])
    n_blk = -(-L // Q_BLOCK)
    Lp = n_blk * Q_BLOCK
    pad = ((0, 0), (0, Lp - L), (0, 0), (0, 0))
    qn_b = jnp.pad(q_nope, pad).reshape(B, n_blk, Q_BLOCK, N_HEADS, QK_NOPE).transpose(1, 0, 2, 3, 4)
    qr_b = jnp.pad(q_rope, pad).reshape(B, n_blk, Q_BLOCK, N_HEADS, QK_ROPE).transpose(1, 0, 2, 3, 4)
    qid_b = chunk_ids(Lp).reshape(n_blk, Q_BLOCK)
    kid = chunk_ids(L)
    sm_scale = 1.0 / math.sqrt(QK_NOPE + QK_ROPE)

    def block(args):
        qn, qr, qid = args
        s = (jnp.einsum('bqhd,bkhd->bhqk', qn, k_nope)
             + jnp.einsum('bqhr,bkr->bhqk', qr, k_rope)).astype(jnp.float32) * sm_scale
        mask = kid[None, :] <= qid[:, None]
        s = jnp.where(mask[None, None], s, jnp.finfo(jnp.float32).min)
        p = jax.nn.softmax(s, axis=-1).astype(v.dtype)
        return jnp.einsum('bhqk,bkhd->bqhd', p, v)

    o = lax.map(block, (qn_b, qr_b, qid_b))
    o = o.transpose(1, 0, 2, 3, 4).reshape(B, Lp, N_HEADS * V_HEAD)[:, :L]
    return o @ w_o


def setup_inputs(seed: int = 0) -> dict:
    key = jax.random.key(seed)
    ks = iter(jax.random.split(key, 40))
    f32 = jnp.float32

    def nrm(shape, fan_in):
        return jax.random.normal(next(ks), shape, f32) * (fan_in ** -0.5)

    def gain(shape):
        return 1.0 + 0.1 * jax.random.normal(next(ks), shape, f32)

    return {
        "x": jax.random.normal(next(ks), (BATCH, SEQ, D_MODEL), f32),
        "meta_tokens": jax.random.normal(next(ks), (N_META, D_MODEL), f32),
        "ffn1_norm": gain((DEPTH, D_MODEL)),
        "ffn1_w_gate": nrm((DEPTH, D_MODEL, D_FF), D_MODEL),
        "ffn1_w_up": nrm((DEPTH, D_MODEL, D_FF), D_MODEL),
        "ffn1_w_down": nrm((DEPTH, D_FF, D_MODEL), D_FF),
        "mix_norm": gain((DEPTH, D_MODEL)),
        "ffn2_norm": gain((DEPTH, D_MODEL)),
        "ffn2_w_gate": nrm((DEPTH, D_MODEL, D_FF), D_MODEL),
        "ffn2_w_up": nrm((DEPTH, D_MODEL, D_FF), D_MODEL),
        "ffn2_w_down": nrm((DEPTH, D_FF, D_MODEL), D_FF),
        "pool_w": nrm((N_A, N_POOL_GROUPS, POOL_GROUP, POOL_GROUP), POOL_GROUP),
        "pool_scale": gain((N_A, D_MODEL)),
        "kv_in_norm": gain((D_MODEL,)),
        "w_dkv": nrm((D_MODEL, KV_RANK + QK_ROPE), D_MODEL),
        "kv_latent_norm": gain((KV_RANK,)),
        "w_uk": nrm((KV_RANK, N_HEADS * QK_NOPE), KV_RANK),
        "w_uv": nrm((KV_RANK, N_HEADS * V_HEAD), KV_RANK),
        "w_dq": nrm((N_B, D_MODEL, Q_RANK), D_MODEL),
        "q_latent_norm": gain((N_B, Q_RANK)),
        "w_uq": nrm((N_B, Q_RANK, N_HEADS * (QK_NOPE + QK_ROPE)), Q_RANK),
        "w_o": nrm((N_B, N_HEADS * V_HEAD, D_MODEL), N_HEADS * V_HEAD),
        "final_norm": gain((D_MODEL,)),
    }


def reference(x, meta_tokens, ffn1_norm, ffn1_w_gate, ffn1_w_up, ffn1_w_down, mix_norm,
              ffn2_norm, ffn2_w_gate, ffn2_w_up, ffn2_w_down, pool_w, pool_scale,
              kv_in_norm, w_dkv, kv_latent_norm, w_uk, w_uv, w_dq, q_latent_norm, w_uq, w_o,
              final_norm):
    B = x.shape[0]
    meta = jnp.broadcast_to(meta_tokens[None].astype(x.dtype), (B, N_META, D_MODEL))
    h = jnp.concatenate([meta, x], axis=1)
    L = h.shape[1]
    cos, sin = rope_tables(L)
    shared = None
    for l in range(DEPTH):
        h = h + 0.5 * swiglu(rmsnorm(h, ffn1_norm[l]), ffn1_w_gate[l], ffn1_w_up[l], ffn1_w_down[l])
        u = rmsnorm(h, mix_norm[l])
        if l < N_A:
            h = h + pool_mixer(u, pool_w[l], pool_scale[l])
        else:
            j = l - N_A
            k_nope, k_rope, v = shared
            h = h + mla_attention(u, w_dq[j], q_latent_norm[j], w_uq[j], w_o[j],
                                  k_nope, k_rope, v, cos, sin)
        h = h + 0.5 * swiglu(rmsnorm(h, ffn2_norm[l]), ffn2_w_gate[l], ffn2_w_up[l], ffn2_w_down[l])
        if l == N_A - 1:
            shared = mla_shared_kv(rmsnorm(h, kv_in_norm), w_dkv, kv_latent_norm, w_uk, w_uv, cos, sin)
    out = rmsnorm(h, final_norm)
    return out[:, N_META:]
```

```python
import contextlib
import numpy as np
import ml_dtypes
import concourse.bass as bass
import concourse.mybir as mybir
from concourse.bass_utils import run_bass_kernel_spmd

F32 = mybir.dt.float32
BF16 = mybir.dt.bfloat16
AF = mybir.ActivationFunctionType
ALU = mybir.AluOpType

D = 1024
DFF = 2816
NJ = DFF // 128
NC_ = 8
HALO = 32
MAIN = 2048
NT = HALO + MAIN
EPS = 1e-6
POOL_W = (2, 4, 8, 16)
NH = 8
SM_SCALE = 1.0 / float(np.sqrt(96.0))
ENGS = ("pe", "act", "dve", "pool", "sp")

G_FFN1, G_MIX, G_FFN2, G_PSC, G_KVIN, G_FIN, G_KVL, G_QL = 0, 32, 64, 96, 112, 120, 128, 130
G_COLS = 136

TILES_A_H = [(0, 32), (32, 512), (544, 512)]
TILES_A = [(32, 512), (544, 512)]
TILES_B = [(1056, 512), (1568, 512)]
TID = {0: 0, 32: 1, 544: 2, 1056: 3, 1568: 4}


class Res:
    __slots__ = ("w", "r")

    def __init__(self):
        self.w = None
        self.r = []


class Slot:
    def __init__(self, sem, scr=False):
        self.sem = sem
        self.count = 0
        self.scr = scr


class Prog:
    def __init__(self, nc, es):
        self.nc = nc
        self.q = {e: [] for e in ENGS}
        self.cnt = {e: 0 for e in ENGS}
        self.sem = {e: es.enter_context(nc.semaphore("s_" + e)) for e in ENGS}
        self.seen = {e: {} for e in ENGS}
        self.hist = {e: {} for e in ENGS}
        self.res = {}
        self.slots = []
        self.es = es
        self.arrive = es.enter_context(nc.semaphore("rst_arrive"))
        self.go = es.enter_context(nc.semaphore("rst_go"))
        self.nreset = 0

    def slot(self, name, scr=False):
        s = Slot(self.es.enter_context(self.nc.semaphore("d_" + name)), scr)
        self.slots.append(s)
        return s

    def _need(self, eng, dep, kind, acc=None):
        if dep is None:
            return
        if dep[0] == "e":
            _, f, tick = dep
            if f == eng:
                if eng in ("pe", "sp"):
                    return
            key = f
            val = tick
            sem = self.sem[f]
        elif dep[0] == "s":
            _, sem, val = dep
            key = ("s", id(sem))
        else:
            _, slot, cnt = dep
            key = slot
            val = max(cnt, slot.count) * 16
            sem = slot.sem
        if self.seen[eng].get(key, 0) >= val:
            return
        if acc is None:
            self.seen[eng][key] = val
            self.q[eng].append(("wait", sem, val))
        else:
            if key not in acc or acc[key][1] < val:
                acc[key] = (sem, val)

    def _deps(self, eng, reads, writes):
        acc = {}
        for k in reads:
            r = self.res.get(k)
            if r is not None:
                self._need(eng, r.w, "raw", acc)
        for k in writes:
            r = self.res.get(k)
            if r is not None:
                self._need(eng, r.w, "waw", acc)
                for d in r.r:
                    self._need(eng, d, "war", acc)
        for key, (sem, val) in acc.items():
            if self.seen[eng].get(key, 0) >= val:
                continue
            self.seen[eng][key] = val
            self.q[eng].append(("wait", sem, val))
            if isinstance(key, str):
                snap = self.hist[key].get(val)
                if snap:
                    mine = self.seen[eng]
                    for k2, v2 in snap.items():
                        if mine.get(k2, 0) < v2:
                            mine[k2] = v2

    def _mark(self, me, reads, writes):
        for k in reads:
            self.res.setdefault(k, Res()).r.append(me)
        for k in writes:
            r = self.res.setdefault(k, Res())
            r.w = me
            r.r = []

    def op(self, eng, fn, reads=(), writes=(), signal=True):
        self._deps(eng, reads, writes)
        tick = self.cnt[eng] + 1
        if signal:
            self.cnt[eng] = tick
            self.hist[eng][tick] = dict(self.seen[eng])
        self.q[eng].append(("op", fn, signal))
        self._mark(("e", eng, tick), reads, writes)

    def dma(self, eng, slot, out, in_, reads=(), writes=()):
        self._deps(eng, reads, writes)
        slot.count += 1
        self.q[eng].append(("dma", out, in_, slot.sem))
        self._mark(("d", slot, slot.count), reads, writes)

    def barrier(self):
        for e in ("pe", "act", "dve", "pool", "sp"):
            for f in ("pe", "act", "dve", "pool"):
                if f != e and self.cnt[f] > 0:
                    self._need(e, ("e", f, self.cnt[f]), "raw")
            for s in self.slots:
                if s.scr and s.count > 0:
                    self._need(e, ("d", s, s.count), "raw")

    def hard_reset(self):
        self.barrier()
        self.nreset += 1
        k = self.nreset
        for e in ("pe", "act", "dve", "sp"):
            self.q[e].append(("arrive",))
        self.q["pool"].append(("wait", self.arrive, 4 * k))
        self.q["pool"].append(("clear",))
        self.q["pool"].append(("go",))
        for e in ("pe", "act", "dve", "sp"):
            self.q[e].append(("wait", self.go, k))
        for e in ENGS:
            self.cnt[e] = 0
            self.hist[e] = {}
            self.seen[e] = {k2: v for k2, v in self.seen[e].items() if not isinstance(k2, str)}
        for r in self.res.values():
            if r.w is not None and r.w[0] == "e":
                r.w = None
            r.r = [d for d in r.r if d[0] != "e"]

    def replay(self, block):
        nc = self.nc
        q = self.q
        sem = self.sem

        def run(ename, eng):
            items = q[ename]
            fold = False
            pend = None
            for idx, it in enumerate(items):
                if it[0] == "wait":
                    if fold and idx + 1 < len(items) and items[idx + 1][0] == "op":
                        pend = (it[1], it[2])
                    else:
                        eng.wait_ge(it[1], it[2])
                elif it[0] == "op" and pend is not None:
                    ins = it[1](eng)
                    ins._wait_ge(pend[0], pend[1])
                    pend = None
                    if it[2]:
                        ins.then_inc(sem[ename], 1)
                elif it[0] == "raw":
                    it[1](eng)
                elif it[0] == "arrive":
                    eng.nop(nofuse=True).then_inc(self.arrive, 1)
                elif it[0] == "go":
                    eng.nop(nofuse=True).then_inc(self.go, 1)
                elif it[0] == "clear":
                    for e2 in ("pe", "act", "dve", "pool"):
                        eng.sem_clear(sem[e2])
                elif it[0] == "op":
                    ins = it[1](eng)
                    if it[2]:
                        ins.then_inc(sem[ename], 1)
                else:
                    eng.dma_start(out=it[1], in_=it[2]).then_inc(it[3], 16)

        @block.tensor
        def _(e):
            run("pe", e)

        @block.scalar
        def _(e):
            run("act", e)

        @block.vector
        def _(e):
            run("dve", e)

        @block.gpsimd
        def _(e):
            run("pool", e)

        @block.sync
        def _(e):
            run("sp", e)


class Unit:
    def __init__(self, kind, groups, layer=None, which=None, xn_reads=None):
        self.kind = kind
        self.groups = groups
        self.layer = layer
        self.which = which
        self.xn_reads = xn_reads if xn_reads is not None else list(groups)
        self.needs_norm = kind in ("ffn", "pool", "mla", "kvlat")


def make_units(cfg):
    part = cfg.get("part", 0)
    nl = cfg["n_layers"]
    layers = range(nl)
    if part == 1:
        layers = range(0, 2)
    elif part == 2:
        layers = range(2, 4)
    units = [Unit("loadh", ["A", "B"])] if part == 2 else [Unit("loadx", ["A", "B"])]
    for l in layers:
        if cfg["ffn"]:
            units += [Unit("ffn", ["A"], l, 1), Unit("ffn", ["B"], l, 1)]
        if cfg["mixer"]:
            if l < 2:
                if cfg.get("pool", True):
                    units += [Unit("pool", ["A"], l), Unit("pool", ["B"], l, xn_reads=["A", "B"])]
            if l >= 2 and cfg.get("mla", True):
                units += [Unit("mla", ["A", "B"], l)]
        if cfg["ffn"]:
            units += [Unit("ffn", ["A"], l, 2), Unit("ffn", ["B"], l, 2)]
        if l == 1 and (nl > 2 or part == 1) and cfg["mixer"]:
            units += [Unit("kvlat", ["A"], l), Unit("kvlat", ["B"], l)]
            if cfg.get("gather", True):
                units += [Unit("gather", [], l)]
    if part == 1:
        units += [Unit("storeh", [])]
    else:
        units += [Unit("final", ["A"]), Unit("final", ["B"])]
    return units


def tiles_of(unit, g):
    halo = unit.kind in ("loadx",) or (unit.layer is not None and unit.layer < 2)
    if g == "A":
        return TILES_A_H if halo else TILES_A
    return TILES_B


def build_nc(cfg):
    nc = bass.Bass("TRN2", target_bir_lowering=False)
    dt = nc.dram_tensor

    def ext(name, shape, dtype=F32):
        return dt(name, list(shape), dtype, kind="ExternalInput").ap()

    xslab = ext("xslab", [NT, D])
    gains = ext("gains", [128, G_COLS])
    ident_d = ext("ident", [128, 128])
    ic_d = ext("ic", [128, 8 * HALO])
    tcs_d = ext("tcs", [32, 2 * NT], BF16)
    valid_d = ext("valid", [128, 21])
    vtile_d = ext("vtile", [128, 81], BF16)
    mask_d = ext("mask", [128, 4 * 512], BF16)
    wg = [ext("ffn1_w_gate", [4, D, DFF]), ext("ffn2_w_gate", [4, D, DFF])]
    wu = [ext("ffn1_w_up", [4, D, DFF]), ext("ffn2_w_up", [4, D, DFF])]
    wd = [ext("ffn1_w_down", [4, DFF, D]), ext("ffn2_w_down", [4, DFF, D])]
    pool_w = ext("pool_w", [2, 4, 256, 256])
    w_dkv = ext("w_dkv", [D, 288])
    w_uk = ext("w_uk", [256, 512])
    w_uv = ext("w_uv", [256, 512])
    w_dq = ext("w_dq", [2, D, 384])
    w_uq = ext("w_uq", [2, 384, 768])
    w_o = ext("w_o", [2, 512, D])
    part = cfg.get("part", 0)
    lshape = [[128, NT], [128, NT], [32, NT]]
    ashape = [[512, NT], [512, NT], [128, NT]]
    if part == 2:
        LIN = [dt("lat_in%d" % i, lshape[i], BF16, kind="ExternalInput") for i in range(3)]
        LALL = [dt("lat_all%d" % i, ashape[i], BF16, kind="ExternalInput") for i in range(3)]
        h_in = dt("h_io", [128, NC_ * MAIN], F32, kind="ExternalInput").ap()
    else:
        LIN = [dt("lat_in%d" % i, lshape[i], BF16) for i in range(3)]
        LALL = [dt("lat_all%d" % i, ashape[i], BF16) for i in range(3)]
    if part == 1:
        LINO = [dt("lat_in%d_o" % i, lshape[i], BF16, kind="ExternalOutput") for i in range(3)]
        LALLO = [dt("lat_all%d_o" % i, ashape[i], BF16, kind="ExternalOutput") for i in range(3)]
        h_out = dt("h_io", [128, NC_ * MAIN], F32, kind="ExternalOutput").ap()
    else:
        out_d = dt("out", [MAIN, D], F32, kind="ExternalOutput").ap()

    units = make_units(cfg)

    with contextlib.ExitStack() as es:
        P = Prog(nc, es)
        sb = lambda name, shape, dtype: es.enter_context(nc.sbuf_tensor(name, list(shape), dtype))
        H = sb("H", [128, NC_, NT], F32)
        SCR = sb("SCR", [128, 39872], BF16)
        RING = [sb("ring%d" % i, [128, 4096], BF16) for i in range(4)]
        SQ = sb("SQ", [128, 8, 512], BF16)
        T1 = sb("T1", [128, 512], F32)
        RSTD = sb("RSTD", [128, 512], F32)
        SG = [sb("SG%d" % i, [128, 512], BF16) for i in range(2)]
        EPSV = sb("EPSV", [128, 1], F32)
        GN = sb("GN", [128, G_COLS], F32)
        ONES = sb("ONES", [128, 128], BF16)
        ONES4 = sb("ONES4", [128, 128], BF16)
        ONES3 = sb("ONES3", [128, 128], BF16)
        ONEF = sb("ONEF", [128, 64], F32)
        IDENT = sb("IDENT", [128, 128], F32)
        TCS = sb("TCS", [32, 2 * NT], BF16)
        IC = sb("IC", [128, 8, HALO], F32)
        VALID = sb("VALID", [128, 21], F32)
        VTILE = sb("VTILE", [128, 81], BF16)
        MASK = sb("MASK", [128, 4, 512], BF16)
        PS = es.enter_context(nc.psum_tensor("PS", [128, 8 * 512], F32))
        cc_sems = [es.enter_context(nc.semaphore("cc_sem%d" % i)) for i in range(3)]

        def bank(i, w=512, p0=0, p1=128):
            return PS[p0:p1, i * 512:i * 512 + w]

        XN = SCR[:, 0:NC_ * NT].rearrange("p (c t) -> p c t", c=NC_)
        AB0 = NC_ * NT
        ACTB = SCR[:, AB0:AB0 + NJ * 1056].rearrange("p (j t) -> p j t", j=NJ)

        def f32view(off_el, n_f32):
            return SCR[:, off_el:off_el + 2 * n_f32].bitcast(F32)

        cslot = P.slot("const")
        consts = [(GN[:], gains[:, :], "GN"), (IDENT[:], ident_d[:, :], "IDENT"),
                  (IC[:].rearrange("p c t -> p (c t)"), ic_d[:, :], "IC"), (TCS[:], tcs_d[:, :], "TCS"),
                  (VALID[:], valid_d[:, :], "VALID"), (VTILE[:], vtile_d[:, :], "VTILE"),
                  (MASK[:].rearrange("p d q -> p (d q)"), mask_d[:, :], "MASK")]
        for o, i_, nm in consts:
            P.dma("sp", cslot, o, i_, writes=[nm])
        for o, i_, nm in consts:
            P.res[nm].w = ("d", cslot, cslot.count)
        for tl, val, nm in ((EPSV, EPS, "EPSV"), (ONES, 1.0 / 1024, "ONES"), (ONES4, 1.0 / 256, "ONES"),
                            (ONES3, 1.0 / 384, "ONES"), (ONEF, 1.0, "ONEF")):
            P.op("dve", (lambda tl, val: (lambda e: e.memset(tl[:], val)))(tl, val), writes=[nm])

        ring_slots = [P.slot("ring%d" % i) for i in range(4)]
        wplan = []

        def plan_weights():
            for u in units:
                if u.kind == "ffn":
                    l, f = u.layer, u.which - 1
                    for sa in range(11):
                        j0 = sa * 256
                        wplan.append((("A", id(u), sa), [
                            (lambda R: R[:, 0:2048].rearrange("p (k j) -> p k j", k=8),
                             wg[f][l].rearrange("(k p) j -> p k j", p=128)[:, :, j0:j0 + 256]),
                            (lambda R: R[:, 2048:4096].rearrange("p (k j) -> p k j", k=8),
                             wu[f][l].rearrange("(k p) j -> p k j", p=128)[:, :, j0:j0 + 256]),
                        ]))
                    for mp in range(4):
                        for kh in range(2):
                            wplan.append((("B", id(u), mp, kh), [
                                (lambda R: R[:, 0:11 * 256].rearrange("p (k j) -> p k j", k=11),
                                 wd[f][l][kh * 1408:(kh + 1) * 1408, :].rearrange(
                                     "(k p) m -> p k m", p=128)[:, :, mp * 256:(mp + 1) * 256]),
                            ]))
                elif u.kind == "pool" and u.groups == ["A"]:
                    wplan.append((("PW", u.layer), [
                        (lambda R: R[:, 0:2048].rearrange("p (g k d) -> p g k d", g=4, k=2),
                         pool_w[u.layer].rearrange("g (k p) d -> p g k d", p=128)),
                    ]))
                elif u.kind == "kvlat" and u.groups == ["A"]:
                    wplan.append((("DKV",), [
                        (lambda R: R[:, 0:8 * 288].rearrange("p (k j) -> p k j", k=8),
                         w_dkv.rearrange("(k p) j -> p k j", p=128)),
                        (lambda R: R[:, 2304:2304 + 256].rearrange("p (k j) -> p k j", k=8)[:, :, 0:16],
                         w_dkv.rearrange("(k p) j -> p k j", p=128)[:, :, 272:288]),
                        (lambda R: R[:, 2304:2304 + 256].rearrange("p (k j) -> p k j", k=8)[:, :, 16:32],
                         w_dkv.rearrange("(k p) j -> p k j", p=128)[:, :, 256:272]),
                    ]))
                elif u.kind == "mla":
                    j = u.layer - 2
                    wplan.append((("DQ", j), [
                        (lambda R: R[:, 0:8 * 384].rearrange("p (k j) -> p k j", k=8),
                         w_dq[j].rearrange("(k p) j -> p k j", p=128)),
                    ]))
                    uq = w_uq[j].rearrange("(k p) (h e) -> p k h e", p=128, e=96)
                    ent = [(lambda R: R[:, 0:3 * 768].rearrange("p (k j) -> p k j", k=3),
                            w_uq[j].rearrange("(k p) j -> p k j", p=128))]
                    for kq in range(3):
                        ent.append(((lambda kq: (lambda R: R[:, 2304:2304 + 768].rearrange(
                            "p (k h e) -> p k h e", k=3, h=8)[:, kq, :, 0:16]))(kq), uq[:, kq, :, 80:96]))
                        ent.append(((lambda kq: (lambda R: R[:, 2304:2304 + 768].rearrange(
                            "p (k h e) -> p k h e", k=3, h=8)[:, kq, :, 16:32]))(kq), uq[:, kq, :, 64:80]))
                    wplan.append((("UQ", j), ent))
                    wplan.append((("UKV", j), [
                        (lambda R: R[:, 0:1024].rearrange("p (k j) -> p k j", k=2),
                         w_uk.rearrange("(k p) j -> p k j", p=128)),
                        (lambda R: R[:, 1024:2048].rearrange("p (k j) -> p k j", k=2),
                         w_uv.rearrange("(k p) j -> p k j", p=128)),
                    ]))
                    wplan.append((("WO", j), [
                        (lambda R: R[:, 0:4096].rearrange("p (k j) -> p k j", k=4),
                         w_o[j].rearrange("(k p) j -> p k j", p=128)),
                    ]))

        plan_weights()
        wpos = {k: i for i, (k, _) in enumerate(wplan)}
        wstate = {"issued": 0}

        def prefetch(upto):
            upto = min(upto, len(wplan) - 1)
            while wstate["issued"] <= upto:
                i = wstate["issued"]
                si = i % 4
                for dstf, src in wplan[i][1]:
                    P.dma("pool", ring_slots[si], dstf(RING[si]), src, writes=[("ring", si)])
                wstate["issued"] += 1

        def wslot(key, ahead=3):
            i = wpos[key]
            prefetch(i + ahead)
            return i % 4, RING[i % 4]

        def ACT(out, in_, func, reads, writes, **kw):
            P.op("act", lambda e: e.activation(out=out, in_=in_, func=func, **kw), reads=reads, writes=writes)

        def TT(eng, out, in0, in1, op, reads, writes):
            P.op(eng, lambda e: e.tensor_tensor(out=out, in0=in0, in1=in1, op=op), reads=reads, writes=writes)

        def STT(out, in0, scalar, in1, op0, op1, reads, writes):
            P.op("dve", lambda e: e.scalar_tensor_tensor(out=out, in0=in0, scalar=scalar, in1=in1,
                                                         op0=op0, op1=op1), reads=reads, writes=writes)

        def TS(eng, out, in0, s1, s2, op0, op1, reads, writes):
            if op1 is None:
                P.op(eng, lambda e: e.tensor_scalar(out=out, in0=in0, scalar1=s1, scalar2=None, op0=op0),
                     reads=reads, writes=writes)
            else:
                P.op(eng, lambda e: e.tensor_scalar(out=out, in0=in0, scalar1=s1, scalar2=s2, op0=op0, op1=op1),
                     reads=reads, writes=writes)

        def COPY(eng, out, in_, reads, writes):
            P.op(eng, lambda e: e.tensor_copy(out=out, in_=in_), reads=reads, writes=writes)

        def MEMSET(eng, out, val, writes):
            P.op(eng, lambda e: e.memset(out, val), writes=writes)

        def TR(out, in_, ident, reads, writes, signal):
            P.op("pe", lambda e: e.transpose(out, in_, ident), reads=reads, writes=writes, signal=signal)
        def mm(out, lhsT, rhs, start, stop, reads, writes, signal=None):
            P.op("pe", lambda e: e.matmul(out, lhsT, rhs, start=start, stop=stop),
                 reads=reads, writes=writes, signal=stop if signal is None else signal)

        sgrot = {"i": 0}

        def RSQ(w, ps_bank):
            ACT(T1[:, 0:w], bank(ps_bank, w), AF.Sqrt, [("ps", ps_bank), "EPSV"], ["T1"], bias=EPSV[:, 0:1])
            P.op("dve", (lambda o_, i_: (lambda e: e.reciprocal(o_, i_)))(RSTD[:, 0:w], T1[:, 0:w]),
                 reads=["T1"], writes=["RSTD"])

        def norm_stats(src_fn, nchunks, ones, w, ps_bank, src_reads):
            for half in range(0, nchunks, 4):
                n = min(4, nchunks - half)
                ACT(SQ[:, 0:n, 0:w], src_fn(half, n), AF.Square, src_reads, [("SQ", 0)])
                for c in range(n):
                    mm(bank(ps_bank, w), ones[:, :], SQ[:, c, 0:w], start=(half + c == 0),
                       stop=(half + c == nchunks - 1), reads=[("SQ", 0), "ONES"], writes=[("ps", ps_bank)],
                       signal=(c == n - 1))
            RSQ(w, ps_bank)

        def norm_steps(u, g):
            if u.kind == "ffn":
                gcol = (G_FFN1 if u.which == 1 else G_FFN2) + 8 * u.layer
            elif u.kind in ("pool", "mla"):
                gcol = G_MIX + 8 * u.layer
            else:
                gcol = G_KVIN
            tiles = tiles_of(u, g)

            def squares(s0, w):
                t = TID[s0]
                ACT(SQ[:, 0:4, 0:w], H[:, 0:4, s0:s0 + w], AF.Square, [("H", t)], [("SQ", 0)])
                ACT(SQ[:, 4:8, 0:w], H[:, 4:8, s0:s0 + w], AF.Square, [("H", t)], [("SQ", 1)])

            def rest(s0, w):
                t = TID[s0]
                for c in range(8):
                    mm(bank(6, w), ONES[:, :], SQ[:, c, 0:w], c == 0, c == 7, [("SQ", c // 4), "ONES"],
                       [("ps", 6)], signal=(c in (3, 7)))
                RSQ(w, 6)
                for c in range(8):
                    STT(XN[:, c, s0:s0 + w], H[:, c, s0:s0 + w], GN[:, gcol + c:gcol + c + 1],
                        RSTD[:, 0:w], ALU.mult, ALU.mult, [("H", t), "RSTD", "GN"], [("XN", t, c)])

            squares(*tiles[0])
            yield
            for idx, (s0, w) in enumerate(tiles):
                rest(s0, w)
                if idx + 1 < len(tiles):
                    squares(*tiles[idx + 1])
                yield

        XN_ALL = lambda t: [("XN", t, c) for c in range(8)]

        def emit_loadx(u):
            XS = [f32view(AB0 + i * 2048, 1024) for i in range(2)]
            xsl = [P.slot("xs%d" % i, scr=True) for i in range(2)]
            subs = [(0, 32)] + [(32 + 128 * i, 128) for i in range(16)]
            for n, (r0, rn) in enumerate(subs):
                b = n % 2
                P.dma("sp", xsl[b], XS[b][0:rn, :], xslab[r0:r0 + rn, :], writes=[("XS", b)])
                tkey = ("H", 0 if r0 == 0 else TID[32 + 512 * ((r0 - 32) // 512)])
                for half in range(2):
                    pb = 6 + half
                    for c in range(4):
                        cc = half * 4 + c
                        TR(PS[:, pb * 512 + c * 128:pb * 512 + c * 128 + rn],
                           XS[b][0:rn, cc * 128:(cc + 1) * 128], IDENT[0:rn, 0:rn],
                           [("XS", b), "IDENT"], [("ps", pb)], c == 3)
                    ACT(H[:, half * 4:half * 4 + 4, r0:r0 + rn],
                        PS[:, pb * 512:(pb + 1) * 512].rearrange("p (c t) -> p c t", c=4)[:, :, 0:rn],
                        AF.Copy, [("ps", pb)], [tkey])
            P.barrier()

        def emit_ffn(u, hook):
            g = u.groups[0]
            tiles = tiles_of(u, g)
            g0 = tiles[0][0]
            rot = 0
            for sa in range(11):
                si, R = wslot(("A", id(u), sa))
                WG = R[:, 0:2048].rearrange("p (k j) -> p k j", k=8)
                WU = R[:, 2048:4096].rearrange("p (k j) -> p k j", k=8)
                for jj in range(2):
                    j = 2 * sa + jj
                    for (s0, w) in tiles:
                        t = TID[s0]
                        bg, bu = 2 * (rot % 3), 2 * (rot % 3) + 1
                        rot += 1
                        for k in range(8):
                            mm(bank(bu, w), WU[:, k, jj * 128:(jj + 1) * 128], XN[:, k, s0:s0 + w],
                               k == 0, k == 7, [("ring", si), ("XN", t, k)], [("ps", bu)])
                        for k in range(8):
                            mm(bank(bg, w), WG[:, k, jj * 128:(jj + 1) * 128], XN[:, k, s0:s0 + w],
                               k == 0, k == 7, [("ring", si), ("XN", t, k)], [("ps", bg)])
                        sgi = sgrot["i"] % 2
                        sgrot["i"] += 1
                        ACT(SG[sgi][:, 0:w], bank(bg, w), AF.Silu, [("ps", bg)], [("SG", sgi)])
                        TT("dve", ACTB[:, j, s0 - g0:s0 - g0 + w], bank(bu, w), SG[sgi][:, 0:w], ALU.mult,
                           [("ps", bu), ("SG", sgi)], [("ACTB", j, t)])
                hook()
            for mp in range(4):
                si0, R0 = wslot(("B", id(u), mp, 0), 3)
                si1, R1 = wslot(("B", id(u), mp, 1), 2)
                WDh = [R0[:, 0:2816].rearrange("p (k j) -> p k j", k=11),
                       R1[:, 0:2816].rearrange("p (k j) -> p k j", k=11)]
                sis = [si0, si1]
                for (s0, w) in tiles:
                    t = TID[s0]
                    for mo in range(2):
                        m = 2 * mp + mo
                        bd = 4 + (rot % 2)
                        rot += 1
                        for k in range(NJ):
                            mm(bank(bd, w), WDh[k // 11][:, k % 11, mo * 128:(mo + 1) * 128],
                               ACTB[:, k, s0 - g0:s0 - g0 + w], k == 0, k == NJ - 1,
                               [("ring", sis[k // 11]), ("ACTB", k, t)], [("ps", bd)])
                        STT(H[:, m, s0:s0 + w], bank(bd, w), 0.5, H[:, m, s0:s0 + w], ALU.mult, ALU.add,
                            [("ps", bd), ("H", t)], [("H", t)])
                hook()

        def emit_pool(u, hook):
            P.barrier()
            g = u.groups[0]
            l = u.layer
            si, R = wslot(("PW", l))
            PW = R[:, 0:2048].rearrange("p (g k d) -> p g k d", g=4, k=2)
            P0 = f32view(AB0, 2 * 544).rearrange("p (c t) -> p c t", c=2)
            P1 = f32view(AB0 + 2176, 2 * 544).rearrange("p (c t) -> p c t", c=2)
            POOLED = SCR[:, AB0 + 4352:AB0 + 4352 + 8 * 512].rearrange("p (c t) -> p c t", c=8)
            for (s0, w) in tiles_of(u, g):
                t = TID[s0]
                for gi, W in enumerate(POOL_W):
                    c0 = 2 * gi
                    eng = "dve"
                    xr = [("XN", t, c0), ("XN", t, c0 + 1)]
                    if s0 == 0:
                        MEMSET(eng, P0[:, :, 0:15], 0.0, ["P0"])
                        COPY(eng, P0[:, :, 15:15 + w], XN[:, c0:c0 + 2, 0:w], xr, ["P0"])
                    else:
                        COPY(eng, P0[:, :, 0:15 + w], XN[:, c0:c0 + 2, s0 - 15:s0 + w],
                             xr + [("XN", t - 1, c0), ("XN", t - 1, c0 + 1)], ["P0"])
                    src, dst, sname, dname = P0, P1, "P0", "P1"
                    sh = 1
                    lo = 0
                    while sh < W:
                        nlo = lo + sh
                        TT(eng, dst[:, :, nlo:15 + w], src[:, :, nlo:15 + w], src[:, :, nlo - sh:15 + w - sh],
                           ALU.add, [sname], [dname])
                        src, dst, sname, dname = dst, src, dname, sname
                        lo = nlo
                        sh *= 2
                    if s0 == 0:
                        TT(eng, dst[:, :, 15:15 + w], src[:, :, 15:15 + w], IC[:, c0:c0 + 2, 0:w], ALU.mult,
                           [sname, "IC"], [dname])
                        TT(eng, POOLED[:, c0:c0 + 2, 0:w], dst[:, :, 15:15 + w], XN[:, c0:c0 + 2, 0:w],
                           ALU.subtract, [dname] + xr, [("POOLED", gi)])
                    else:
                        STT(POOLED[:, c0:c0 + 2, 0:w], src[:, :, 15:15 + w], 1.0 / W,
                            XN[:, c0:c0 + 2, s0:s0 + w], ALU.mult, ALU.subtract, [sname] + xr, [("POOLED", gi)])
                    for mc in range(2):
                        dch = c0 + mc
                        pb = 6 + (dch % 2)
                        for k in range(2):
                            mm(bank(pb, w), PW[:, gi, k, mc * 128:(mc + 1) * 128], POOLED[:, c0 + k, 0:w],
                               k == 0, k == 1, [("ring", si), ("POOLED", gi)], [("ps", pb)])
                        STT(H[:, dch, s0:s0 + w], bank(pb, w),
                            GN[:, G_PSC + 8 * l + dch:G_PSC + 8 * l + dch + 1], H[:, dch, s0:s0 + w],
                            ALU.mult, ALU.add, [("ps", pb), ("H", t), "GN"], [("H", t)])
            P.barrier()

        def emit_final(u):
            P.barrier()
            g = u.groups[0]
            YN = f32view(AB0, 8 * 512).rearrange("p (c t) -> p c t", c=8)
            OS = [f32view(AB0 + 8192 + i * 2048, 1024) for i in range(2)]
            osl = [P.slot("os%s%d" % (g, i), scr=True) for i in range(2)]
            n = 0
            for (s0, w) in tiles_of(u, g):
                t = TID[s0]
                norm_stats(lambda half, n_: H[:, half:half + n_, s0:s0 + w], 8, ONES, w, 6, [("H", t)])
                for c in range(8):
                    STT(YN[:, c, 0:w], H[:, c, s0:s0 + w], GN[:, G_FIN + c:G_FIN + c + 1], RSTD[:, 0:w],
                        ALU.mult, ALU.mult, [("H", t), "RSTD", "GN"], [("YN", c)])
                for sub in range(w // 128):
                    b = n % 2
                    n += 1
                    for half in range(2):
                        pb = 4 + half
                        for c in range(4):
                            cc = half * 4 + c
                            TR(PS[:, pb * 512 + c * 128:pb * 512 + (c + 1) * 128],
                               YN[:, cc, sub * 128:(sub + 1) * 128], IDENT[:, :],
                               [("YN", cc), "IDENT"], [("ps", pb)], c == 3)
                        if half == 0:
                            ACT(OS[b][:, 0:512], bank(pb), AF.Copy, [("ps", pb)], [("OS", b, 0)])
                        else:
                            COPY("dve", OS[b][:, 512:1024], bank(pb), [("ps", pb)], [("OS", b, 1)])
                    r0 = s0 - HALO + sub * 128
                    P.dma("sp", osl[b], out_d[r0:r0 + 128, :], OS[b][:, :], reads=[("OS", b, 0), ("OS", b, 1)])
            P.barrier()
            return osl

        def emit_kvlat(u, hook):
            P.barrier()
            g = u.groups[0]
            si, R = wslot(("DKV",))
            WD = R[:, 0:2304].rearrange("p (k j) -> p k j", k=8)
            WS = R[:, 2304:2560].rearrange("p (k j) -> p k j", k=8)
            SQK = SCR[:, AB0:AB0 + 1024].rearrange("p (c t) -> p c t", c=2)
            LATS = [SCR[:, AB0 + 1024 + i * 1024:AB0 + 2048 + i * 1024].rearrange("p (c t) -> p c t", c=2)
                    for i in range(2)]
            KR = [SCR[:, AB0 + 3072 + i * 512:AB0 + 3584 + i * 512] for i in range(2)]
            F1 = f32view(AB0 + 4096, 512)
            F2 = f32view(AB0 + 5120, 512)
            lsl = [P.slot("lat%s%d" % (g, i), scr=True) for i in range(2)]
            for n, (s0, w) in enumerate(tiles_of(u, g)):
                t = TID[s0]
                b = n % 2
                xr = [("XN", t, k) for k in range(8)]
                for mc in range(2):
                    for k in range(8):
                        mm(bank(mc, w), WD[:, k, mc * 128:(mc + 1) * 128], XN[:, k, s0:s0 + w], k == 0, k == 7,
                           [("ring", si), ("XN", t, k)], [("ps", mc)])
                for k in range(8):
                    mm(bank(2, w, 0, 32), WD[:, k, 256:288], XN[:, k, s0:s0 + w], k == 0, k == 7,
                       [("ring", si), ("XN", t, k)], [("ps", 2)])
                for k in range(8):
                    mm(bank(3, w, 0, 32), WS[:, k, :], XN[:, k, s0:s0 + w], k == 0, k == 7,
                       [("ring", si), ("XN", t, k)], [("ps", 3)])
                for mc in range(2):
                    ACT(SQK[:, mc, 0:w], bank(mc, w), AF.Square, [("ps", mc)], [("SQK", mc)])
                for mc in range(2):
                    mm(bank(6, w), ONES4[:, :], SQK[:, mc, 0:w], mc == 0, mc == 1, [("SQK", mc), "ONES"],
                       [("ps", 6)])
                RSQ(w, 6)
                for mc in range(2):
                    STT(LATS[b][:, mc, 0:w], bank(mc, w), GN[:, G_KVL + mc:G_KVL + mc + 1], RSTD[:, 0:w],
                        ALU.mult, ALU.mult, [("ps", mc), "RSTD", "GN"], [("LATS", b)])
                TT("dve", F1[0:32, 0:w], bank(2, w, 0, 32), TCS[0:32, s0:s0 + w], ALU.mult,
                   [("ps", 2), "TCS"], ["F1"])
                TT("dve", F2[0:32, 0:w], bank(3, w, 0, 32), TCS[0:32, NT + s0:NT + s0 + w], ALU.mult,
                   [("ps", 3), "TCS"], ["F2"])
                TT("dve", KR[b][0:32, 0:w], F1[0:32, 0:w], F2[0:32, 0:w], ALU.add, ["F1", "F2"], [("KR", b)])
                for mc in range(2):
                    P.dma("sp", lsl[b], LIN[mc][:, s0:s0 + w], LATS[b][:, mc, 0:w], reads=[("LATS", b)])
                P.dma("sp", lsl[b], LIN[2][:, s0:s0 + w], KR[b][0:32, 0:w], reads=[("KR", b)])
            P.barrier()

        def emit_gather(u):
            P.barrier()

            def mk(i):
                def fn(e):
                    e.collective_compute("AllGather", ALU.bypass, replica_groups=[[0, 1, 2, 3], [4, 5, 6, 7]],
                                         ins=[LIN[i].ap().opt()], outs=[LALL[i].ap().opt()]).then_inc(cc_sems[i])
                return fn
            for i in range(3):
                P.q["pool"].append(("raw", mk(i)))
                P.res.setdefault(("LATALL", i), Res()).w = ("s", cc_sems[i], 1)

        QT = [(32, 512), (544, 512), (1056, 512), (1568, 512)]

        def emit_mla(u):
            j = u.layer - 2
            P.barrier()
            o = AB0
            CQ = SCR[:, o:o + 6144].rearrange("p (c t) -> p c t", c=3); o += 6144
            OT = SCR[:, o:o + 8192].rearrange("p (c t) -> p c t", c=4); o += 8192
            QH = SCR[:, o:o + 2048]; o += 2048
            PT = [SCR[:, o + i * 1024:o + (i + 1) * 1024] for i in range(2)]; o += 2048
            NCL = 3
            CL = [SCR[:, o + i * 1024:o + (i + 1) * 1024].rearrange("p (c t) -> p c t", c=2) for i in range(2)]
            o += 2048
            CL.append(SCR[:, 10256 + 81 * 65:10256 + 81 * 65 + 1024].rearrange("p (c t) -> p c t", c=2))
            assert 10256 + 81 * 65 + 1024 <= AB0
            F1 = f32view(o, 512); o += 1024
            F2 = f32view(o, 512); o += 1024
            assert o <= AB0 + NJ * 1056
            KH = SCR[:, 0:10256]
            VA = SCR[:, 10256:10256 + 81 * 65].rearrange("p (t d) -> p t d", d=65)
            assert 10256 + 81 * 65 <= AB0
            si_dq, RDQ = wslot(("DQ", j))
            WDQ = RDQ[:, 0:3072].rearrange("p (k j) -> p k j", k=8)
            for (s0, w) in QT:
                t = TID[s0]
                m0 = s0 - HALO
                for mc in range(3):
                    for k in range(8):
                        mm(bank(mc), WDQ[:, k, mc * 128:(mc + 1) * 128], XN[:, k, s0:s0 + 512], k == 0, k == 7,
                           [("ring", si_dq), ("XN", t, k)], [("ps", mc)])
                ACT(SQ[:, 0:3, :], PS[:, 0:1536].rearrange("p (c t) -> p c t", c=3), AF.Square,
                    [("ps", 0), ("ps", 1), ("ps", 2)], [("SQ", 0)])
                for mc in range(3):
                    mm(bank(6), ONES3[:, :], SQ[:, mc, :], mc == 0, mc == 2, [("SQ", 0), "ONES"], [("ps", 6)])
                RSQ(512, 6)
                for mc in range(3):
                    STT(CQ[:, mc, m0:m0 + 512], bank(mc), GN[:, G_QL + 3 * j + mc:G_QL + 3 * j + mc + 1],
                        RSTD[:, :], ALU.mult, ALU.mult, [("ps", mc), "RSTD", "GN"], [("CQ", t)])
            P.barrier()
            si_uq, RUQ = wslot(("UQ", j), 3)
            WUQ = RUQ[:, 0:2304].rearrange("p (k j) -> p k j", k=3)
            WUQS = RUQ[:, 2304:3072].rearrange("p (k h e) -> p k h e", k=3, h=8)
            si_kv, RKV = wslot(("UKV", j), 2)
            WUK = RKV[:, 0:1024].rearrange("p (k j) -> p k j", k=2)
            WUV = RKV[:, 1024:2048].rearrange("p (k j) -> p k j", k=2)
            si_o, RO = wslot(("WO", j), 1)
            WO = RO[:, 0:4096].rearrange("p (k j) -> p k j", k=4)
            COPY("dve", VA[:, :, 64], VTILE[:, :], ["VTILE"], ["VAONE"])
            clsl = [P.slot("cl%d_%d" % (j, i), scr=True) for i in range(NCL)]
            khsl = P.slot("khr%d" % j, scr=True)
            nblk = 0
            for h in range(NH):
                kh_keys = []
                for blk in range(21):
                    rd = [("LATALL", 0), ("LATALL", 1), ("LATALL", 2)]
                    if blk == 0:
                        srcs = [LALL[0][0:128, 16:32], LALL[1][0:128, 16:32], LALL[2][0:32, 16:32]]
                        nk, kc0, vt0 = 16, 0, 0
                    elif blk <= 16:
                        r, q4 = (blk - 1) // 4, (blk - 1) % 4
                        c0_, c1_ = HALO + 512 * q4, HALO + 512 * q4 + 512
                        srcs = [LALL[0][128 * r:128 * r + 128, c0_:c1_], LALL[1][128 * r:128 * r + 128, c0_:c1_],
                                LALL[2][32 * r:32 * r + 32, c0_:c1_]]
                        nk, kc0, vt0 = 512, 16 + 512 * (blk - 1), 1 + 4 * (blk - 1)
                    else:
                        q4 = blk - 17
                        c0_, c1_ = HALO + 512 * q4, HALO + 512 * q4 + 512
                        srcs = [LIN[0][:, c0_:c1_], LIN[1][:, c0_:c1_], LIN[2][:, c0_:c1_]]
                        nk, kc0, vt0 = 512, 16 + 8192 + 512 * q4, 65 + 4 * q4
                    b = nblk % NCL
                    nblk += 1
                    for c in range(2):
                        P.dma("sp" if c == 0 else "act", clsl[b], CL[b][:, c, 0:nk], srcs[c], reads=rd,
                              writes=[("CL", b)])
                    if h == 0:
                        P.dma("sp", khsl, KH[64:96, kc0:kc0 + nk], srcs[2], reads=rd, writes=[("KHR", blk)])
                        kh_keys.append(("KHR", blk))
                    nb = blk % 2
                    vb = 2 + (blk % 2)
                    hp, ho = h // 2, (h % 2) * 64
                    for c in range(2):
                        mm(bank(nb, nk), WUK[:, c, hp * 128:(hp + 1) * 128], CL[b][:, c, 0:nk], c == 0, c == 1,
                           [("ring", si_kv), ("CL", b)], [("ps", nb)])
                    COPY("dve", KH[0:64, kc0:kc0 + nk], bank(nb, nk, ho, ho + 64), [("ps", nb)], [("KHN", blk)])
                    kk = min(128, nk)
                    nt = max(1, nk // 128)
                    for ti in range(nt):
                        for c in range(2):
                            mm(PS[0:kk, vb * 512 + ti * 64:vb * 512 + ti * 64 + 64],
                               CL[b][:, c, ti * 128:ti * 128 + kk], WUV[:, c, h * 64:(h + 1) * 64], c == 0, c == 1,
                               [("ring", si_kv), ("CL", b)], [("ps", vb)], signal=(c == 1 and ti == nt - 1))
                    TS("dve", VA[0:kk, vt0:vt0 + nt, 0:64],
                       PS[0:kk, vb * 512:vb * 512 + nt * 64].rearrange("p (t d) -> p t d", d=64),
                       VALID[0:kk, blk:blk + 1], None, ALU.mult, None, [("ps", vb), "VALID"], [("VA", blk)])
                for kkey in kh_keys:
                    P.res[kkey].w = ("d", khsl, khsl.count)
                for (s0, w) in QT:
                    t = TID[s0]
                    m0 = s0 - HALO
                    for c in range(3):
                        mm(bank(5, 512, 0, 64), WUQ[:, c, h * 96:h * 96 + 64], CQ[:, c, m0:m0 + 512], c == 0, c == 2,
                           [("ring", si_uq), ("CQ", t)], [("ps", 5)])
                    for c in range(3):
                        mm(bank(6, 512, 0, 32), WUQ[:, c, h * 96 + 64:h * 96 + 96], CQ[:, c, m0:m0 + 512], c == 0,
                           c == 2, [("ring", si_uq), ("CQ", t)], [("ps", 6)])
                    for c in range(3):
                        mm(bank(7, 512, 0, 32), WUQS[:, c, h, :], CQ[:, c, m0:m0 + 512], c == 0, c == 2,
                           [("ring", si_uq), ("CQ", t)], [("ps", 7)])
                    ACT(QH[0:64, m0:m0 + 512], bank(5, 512, 0, 64), AF.Copy, [("ps", 5)], [("QH", t)])
                    TT("dve", F1[0:32, :], bank(6, 512, 0, 32), TCS[0:32, s0:s0 + 512], ALU.mult,
                       [("ps", 6), "TCS"], ["F1"])
                    TT("dve", F2[0:32, :], bank(7, 512, 0, 32), TCS[0:32, NT + s0:NT + s0 + 512], ALU.mult,
                       [("ps", 7), "TCS"], ["F2"])
                    TT("dve", QH[64:96, m0:m0 + 512], F1[0:32, :], F2[0:32, :], ALU.add, ["F1", "F2"],
                       [("QHR", t)])
                pend_norm = []
                for qi, (s0, w) in enumerate(QT):
                    t = TID[s0]
                    m0 = s0 - HALO
                    ktl = [(0, 16, 0, None, 0)]
                    for kt in range(64):
                        ktl.append((16 + 128 * kt, 128, 1 + kt, None, 1 + kt // 4))
                    for kt in range(4 * (qi + 1)):
                        dd = kt - 4 * qi
                        ktl.append((16 + 8192 + 128 * kt, 128, 65 + kt, dd if dd >= 0 else None, 17 + kt // 4))
                    ob = 4 + (qi % 2)
                    psO = PS[0:65, ob * 512:(ob + 1) * 512]
                    nkt = len(ktl)

                    def scores_pair(p, slot):
                        for half in range(2):
                            kc, nk, vt, dd, blk = ktl[1 + 2 * p + half]
                            sbk = 2 * slot + half
                            mm(bank(sbk, 512, 0, nk), KH[0:96, kc:kc + nk], QH[0:96, m0:m0 + 512], True, True,
                               [("KHR", blk), ("KHN", blk), ("QH", t), ("QHR", t)], [("ps", sbk)])

                    def expv_pair(p, slot):
                        pt = PT[slot]
                        ACT(pt[:, 0:1024], PS[:, 2 * slot * 512:2 * slot * 512 + 1024], AF.Exp,
                            [("ps", 2 * slot), ("ps", 2 * slot + 1)], [("PT", slot)], scale=SM_SCALE)
                        for half in range(2):
                            kc, nk, vt, dd, blk = ktl[1 + 2 * p + half]
                            if dd is not None:
                                TT("dve", pt[:, half * 512:(half + 1) * 512], pt[:, half * 512:(half + 1) * 512],
                                   MASK[:, dd, :], ALU.mult, [("PT", slot), "MASK"], [("PT", slot)])
                        for half in range(2):
                            kc, nk, vt, dd, blk = ktl[1 + 2 * p + half]
                            last = (1 + 2 * p + half == nkt - 1)
                            mm(psO, VA[0:nk, vt, 0:65], pt[0:nk, half * 512:(half + 1) * 512], False, last,
                               [("VA", blk), "VAONE", ("PT", slot)], [("ps", ob)], signal=True)

                    def meta_scores():
                        kc, nk, vt, dd, blk = ktl[0]
                        mm(bank(2, 512, 0, nk), KH[0:96, kc:kc + nk], QH[0:96, m0:m0 + 512], True, True,
                           [("KHR", blk), ("KHN", blk), ("QH", t), ("QHR", t)], [("ps", 2)])

                    def meta_expv():
                        kc, nk, vt, dd, blk = ktl[0]
                        pt = PT[1]
                        ACT(pt[0:nk, 0:512], bank(2, 512, 0, nk), AF.Exp, [("ps", 2)], [("PT", 1)], scale=SM_SCALE)
                        mm(psO, VA[0:nk, vt, 0:65], pt[0:nk, 0:512], True, False,
                           [("VA", blk), "VAONE", ("PT", 1)], [("ps", ob)], signal=True)

                    def normalise(ob=ob, m0=m0, t=t, h=h):
                        P.op("dve", (lambda o_, i_: (lambda e: e.reciprocal(o_, i_)))(
                            F2[64:65, :], PS[64:65, ob * 512:(ob + 1) * 512]), reads=[("ps", ob)], writes=["F2"])
                        mm(bank(6, 512, 0, 64), ONEF[64:65, 0:64], F2[64:65, :], True, True, ["F2", "ONEF"],
                           [("ps", 6)])
                        ACT(F1[0:64, :], PS[0:64, ob * 512:(ob + 1) * 512], AF.Copy, [("ps", ob)], ["F1"])
                        p0 = (h % 2) * 64
                        TT("dve", OT[p0:p0 + 64, h // 2, m0:m0 + 512], F1[0:64, :], bank(6, 512, 0, 64), ALU.mult,
                           ["F1", ("ps", 6)], [("OT", h // 2, t)])

                    npair = (nkt - 1) // 2
                    assert 1 + 2 * npair == nkt
                    meta_scores()
                    scores_pair(0, 0)
                    meta_expv()
                    for p in range(1, npair):
                        scores_pair(p, p % 2)
                        expv_pair(p - 1, (p - 1) % 2)
                        if p == 3 and pend_norm:
                            pend_norm.pop()()
                    expv_pair(npair - 1, (npair - 1) % 2)
                    if pend_norm:
                        pend_norm.pop()()
                    pend_norm.append(normalise)
                if pend_norm:
                    pend_norm.pop()()
            rot = 0
            for (s0, w) in QT:
                t = TID[s0]
                m0 = s0 - HALO
                for m in range(8):
                    pb = rot % 3
                    rot += 1
                    for p_ in range(4):
                        mm(bank(pb), WO[:, p_, m * 128:(m + 1) * 128], OT[:, p_, m0:m0 + 512], p_ == 0, p_ == 3,
                           [("ring", si_o), ("OT", p_, t)], [("ps", pb)])
                    TT("dve", H[:, m, s0:s0 + 512], bank(pb), H[:, m, s0:s0 + 512], ALU.add,
                       [("ps", pb), ("H", t)], [("H", t)])
            P.barrier()

        def emit_storeh(u):
            P.barrier()
            sl = P.slot("storeh", scr=True)
            rd = [("LATALL", 0), ("LATALL", 1), ("LATALL", 2)] + [("H", t) for t in range(1, 5)]
            for c in range(8):
                P.dma("sp", sl, h_out[:, c * MAIN:(c + 1) * MAIN], H[:, c, HALO:NT], reads=rd)
            for i in range(3):
                P.dma("sp", sl, LINO[i][:, :], LIN[i][:, :], reads=rd)
                P.dma("sp", sl, LALLO[i][:, :], LALL[i][:, :], reads=rd)
            return [sl]

        def emit_loadh(u):
            sl = P.slot("loadh", scr=True)
            for c in range(8):
                P.dma("sp", sl, H[:, c, HALO:NT], h_in[:, c * MAIN:(c + 1) * MAIN],
                      writes=[("H", t) for t in range(1, 5)])
            P.barrier()

        normed = set()
        emitted = set()

        def last_h_writer(idx, g):
            for i in range(idx - 1, -1, -1):
                if g in units[i].groups and units[i].kind in ("loadx", "loadh", "ffn", "pool", "mla"):
                    return i
            return -1

        pending = {}

        def ensure_norm(idx, g):
            u = units[idx]
            if not u.needs_norm:
                return
            if (idx, g) not in normed:
                normed.add((idx, g))
                pending[(idx, g)] = norm_steps(u, g)
            gen = pending.pop((idx, g), None)
            if gen is not None:
                for _ in gen:
                    pass

        def advance():
            for key in list(pending.keys()):
                gen = pending[key]
                try:
                    next(gen)
                except StopIteration:
                    pending.pop(key, None)
                break

        def try_ahead(i):
            for vi in range(i + 1, min(i + 3, len(units))):
                v = units[vi]
                if not v.needs_norm:
                    continue
                for g in v.groups:
                    if (vi, g) in normed:
                        continue
                    if last_h_writer(vi, g) >= i:
                        continue
                    blocked = any(g in units[k].xn_reads and k not in emitted for k in range(0, vi))
                    if blocked:
                        continue
                    normed.add((vi, g))
                    pending[(vi, g)] = norm_steps(v, g)
            advance()

        final_slots = []
        prefetch(2)
        for i, u in enumerate(units):
            for g in u.groups:
                ensure_norm(i, g)
            hook = (lambda i=i: try_ahead(i))
            if u.kind == "loadx":
                emit_loadx(u)
            elif u.kind == "loadh":
                emit_loadh(u)
            elif u.kind == "storeh":
                final_slots += emit_storeh(u)
            elif u.kind == "ffn":
                emit_ffn(u, hook)
            elif u.kind == "pool":
                emit_pool(u, hook)
            elif u.kind == "kvlat":
                emit_kvlat(u, hook)
            elif u.kind == "gather":
                emit_gather(u)
                if cfg.get("part", 0) == 0:
                    P.hard_reset()
            elif u.kind == "mla":
                emit_mla(u)
            elif u.kind == "final":
                final_slots += emit_final(u)
            else:
                raise NotImplementedError(u.kind)
            emitted.add(i)

        for s in final_slots:
            if s.count:
                P._need("sp", ("d", s, s.count), "raw")
                P._need("pe", ("d", s, s.count), "raw")

        with nc.Block() as block:
            P.replay(block)
    return nc


def rope_tables(pos):
    inv = (1.0 / (10000.0 ** (np.arange(0, 32, 2, dtype=np.float32) / np.float32(32)))).astype(np.float32)
    ang = pos.astype(np.float32)[:, None] * inv[None, :]
    return np.cos(ang).astype(np.float32), np.sin(ang).astype(np.float32)


def host_inputs(inputs, n_cores=8):
    x = np.asarray(inputs["x"], dtype=np.float32)
    meta = np.asarray(inputs["meta_tokens"], dtype=np.float32)

    def cols(a):
        a = np.asarray(a, dtype=np.float32)
        a = a.reshape(a.shape[0], -1, 128)
        return np.ascontiguousarray(a.transpose(2, 0, 1).reshape(128, -1))

    gains = np.concatenate([
        cols(inputs["ffn1_norm"]), cols(inputs["mix_norm"]), cols(inputs["ffn2_norm"]),
        cols(inputs["pool_scale"]), cols(np.asarray(inputs["kv_in_norm"])[None]),
        cols(np.asarray(inputs["final_norm"])[None]), cols(np.asarray(inputs["kv_latent_norm"])[None]),
        cols(inputs["q_latent_norm"]),
    ], axis=1)
    assert gains.shape == (128, G_COLS)
    ident = np.eye(128, dtype=np.float32)
    k = np.arange(128)[:, None, None]
    d = np.arange(4)[None, :, None]
    q = np.arange(512)[None, None, :]
    mask = ((128 * d + k) // 64 <= q // 64).astype(np.float32).reshape(128, 2048).astype(ml_dtypes.bfloat16)
    shared = {
        "gains": gains, "ident": ident, "mask": mask,
    }
    for name in ("ffn1_w_gate", "ffn1_w_up", "ffn1_w_down", "ffn2_w_gate", "ffn2_w_up", "ffn2_w_down",
                 "pool_w", "w_dkv", "w_uk", "w_uv", "w_dq", "w_uq", "w_o"):
        shared[name] = np.ascontiguousarray(np.asarray(inputs[name], dtype=np.float32))
    maps = []
    for i in range(n_cores):
        b, c = i // 4, i % 4
        slab = np.zeros((NT, D), np.float32)
        if c == 0:
            slab[16:32] = meta
            slab[32:] = x[b, 0:MAIN]
        else:
            slab[:] = x[b, MAIN * c - HALO:MAIN * c + MAIN]
        pos = (MAIN * c - 16 + np.arange(NT)).astype(np.float32)
        cs, sn = rope_tables(np.maximum(pos, 0))
        tc = np.concatenate([cs.T, cs.T], axis=0)
        ts = np.concatenate([-sn.T, sn.T], axis=0)
        tcs = np.concatenate([tc, ts], axis=1).astype(ml_dtypes.bfloat16)
        ic = np.zeros((128, 8, HALO), np.float32)
        for ch in range(8):
            W = POOL_W[ch // 2]
            ic[:, ch, :] = 1.0 / W
            if c == 0:
                for s in range(16, 32):
                    ic[:, ch, s] = 1.0 / min(W, s - 16 + 1)
        valid = np.zeros((128, 21), np.float32)
        valid[:, 0] = 1.0
        for blk in range(1, 17):
            valid[:, blk] = 1.0 if (blk - 1) // 4 < c else 0.0
        valid[:, 17:] = 1.0
        vtile = np.zeros((128, 81), np.float32)
        vtile[:, 0] = 1.0
        for blk in range(1, 21):
            vtile[:, 1 + 4 * (blk - 1):1 + 4 * blk] = valid[:, blk:blk + 1]
        m = dict(shared)
        m.update({"xslab": slab, "tcs": tcs, "ic": ic.reshape(128, 8 * HALO), "valid": valid,
                  "vtile": vtile.astype(ml_dtypes.bfloat16)})
        maps.append(m)
    return maps


FULL_CFG = {"n_layers": 4, "ffn": True, "mixer": True}
_CACHE = {}


def get_nc(cfg):
    key = tuple(sorted(cfg.items()))
    if key not in _CACHE:
        _CACHE[key] = build_nc(cfg)
    return _CACHE[key]


def run(inputs, cfg):
    maps = host_inputs(inputs)
    if cfg.get("split", False):
        c1 = dict(cfg); c1.pop("split"); c1["part"] = 1
        c2 = dict(c1); c2["part"] = 2
        r1 = run_bass_kernel_spmd(get_nc(c1), maps, core_ids=list(range(8)))
        maps2 = []
        for i in range(8):
            m = dict(maps[i])
            o = r1.results[i]
            m["h_io"] = o["h_io"]
            for k in range(3):
                m["lat_in%d" % k] = o["lat_in%d_o" % k]
                m["lat_all%d" % k] = o["lat_all%d_o" % k]
            maps2.append(m)
        res = run_bass_kernel_spmd(get_nc(c2), maps2, core_ids=list(range(8)))
    else:
        res = run_bass_kernel_spmd(get_nc(cfg), maps, core_ids=list(range(8)))
    out = np.zeros((2, 8192, D), np.float32)
    for i in range(8):
        b, c = i // 4, i % 4
        out[b, MAIN * c:MAIN * (c + 1)] = res.results[i]["out"]
    return out


def kernel(**inputs):
    cfg = dict(FULL_CFG)
    return run(inputs, cfg)
```

```python
import contextlib
import numpy as np
import ml_dtypes
import concourse.bass as bass
import concourse.mybir as mybir
from concourse.bass_utils import run_bass_kernel_spmd

F32 = mybir.dt.float32
BF16 = mybir.dt.bfloat16
AF = mybir.ActivationFunctionType
ALU = mybir.AluOpType

D = 1024
DFF = 2816
NJ = DFF // 128
NC_ = 8
HALO = 32
MAIN = 2048
NT = HALO + MAIN
EPS = 1e-6
POOL_W = (2, 4, 8, 16)
NH = 8
SM_SCALE = 1.0 / float(np.sqrt(96.0))
ENGS = ("pe", "act", "dve", "pool", "sp")

G_FFN1, G_MIX, G_FFN2, G_PSC, G_KVIN, G_FIN, G_KVL, G_QL = 0, 32, 64, 96, 112, 120, 128, 130
G_COLS = 136

TILES_A_H = [(0, 32), (32, 512), (544, 512)]
TILES_A = [(32, 512), (544, 512)]
TILES_B = [(1056, 512), (1568, 512)]
TID = {0: 0, 32: 1, 544: 2, 1056: 3, 1568: 4}


class Res:
    __slots__ = ("w", "r")

    def __init__(self):
        self.w = None
        self.r = []


class Slot:
    def __init__(self, sem, scr=False):
        self.sem = sem
        self.count = 0
        self.scr = scr


class Prog:
    def __init__(self, nc, es):
        self.nc = nc
        self.q = {e: [] for e in ENGS}
        self.cnt = {e: 0 for e in ENGS}
        self.sem = {e: es.enter_context(nc.semaphore("s_" + e)) for e in ENGS}
        self.seen = {e: {} for e in ENGS}
        self.hist = {e: {} for e in ENGS}
        self.res = {}
        self.slots = []
        self.es = es
        self.arrive = es.enter_context(nc.semaphore("rst_arrive"))
        self.go = es.enter_context(nc.semaphore("rst_go"))
        self.nreset = 0

    def slot(self, name, scr=False):
        s = Slot(self.es.enter_context(self.nc.semaphore("d_" + name)), scr)
        self.slots.append(s)
        return s

    def _need(self, eng, dep, kind, acc=None):
        if dep is None:
            return
        if dep[0] == "e":
            _, f, tick = dep
            if f == eng:
                if eng in ("pe", "sp"):
                    return
            key = f
            val = tick
            sem = self.sem[f]
        elif dep[0] == "s":
            _, sem, val = dep
            key = ("s", id(sem))
        else:
            _, slot, cnt = dep
            key = slot
            val = max(cnt, slot.count) * 16
            sem = slot.sem
        if self.seen[eng].get(key, 0) >= val:
            return
        if acc is None:
            self.seen[eng][key] = val
            self.q[eng].append(("wait", sem, val))
        else:
            if key not in acc or acc[key][1] < val:
                acc[key] = (sem, val)

    def _deps(self, eng, reads, writes):
        acc = {}
        for k in reads:
            r = self.res.get(k)
            if r is not None:
                self._need(eng, r.w, "raw", acc)
        for k in writes:
            r = self.res.get(k)
            if r is not None:
                self._need(eng, r.w, "waw", acc)
                for d in r.r:
                    self._need(eng, d, "war", acc)
        for key, (sem, val) in acc.items():
            if self.seen[eng].get(key, 0) >= val:
                continue
            self.seen[eng][key] = val
            self.q[eng].append(("wait", sem, val))
            if isinstance(key, str):
                snap = self.hist[key].get(val)
                if snap:
                    mine = self.seen[eng]
                    for k2, v2 in snap.items():
                        if mine.get(k2, 0) < v2:
                            mine[k2] = v2

    def _mark(self, me, reads, writes):
        for k in reads:
            self.res.setdefault(k, Res()).r.append(me)
        for k in writes:
            r = self.res.setdefault(k, Res())
            r.w = me
            r.r = []

    def op(self, eng, fn, reads=(), writes=(), signal=True):
        self._deps(eng, reads, writes)
        tick = self.cnt[eng] + 1
        if signal:
            self.cnt[eng] = tick
            self.hist[eng][tick] = dict(self.seen[eng])
        self.q[eng].append(("op", fn, signal))
        self._mark(("e", eng, tick), reads, writes)

    def dma(self, eng, slot, out, in_, reads=(), writes=()):
        self._deps(eng, reads, writes)
        slot.count += 1
        self.q[eng].append(("dma", out, in_, slot.sem))
        self._mark(("d", slot, slot.count), reads, writes)

    def barrier(self):
        for e in ("pe", "act", "dve", "pool", "sp"):
            for f in ("pe", "act", "dve", "pool"):
                if f != e and self.cnt[f] > 0:
                    self._need(e, ("e", f, self.cnt[f]), "raw")
            for s in self.slots:
                if s.scr and s.count > 0:
                    self._need(e, ("d", s, s.count), "raw")

    def hard_reset(self):
        self.barrier()
        self.nreset += 1
        k = self.nreset
        for e in ("pe", "act", "dve", "sp"):
            self.q[e].append(("arrive",))
        self.q["pool"].append(("wait", self.arrive, 4 * k))
        self.q["pool"].append(("clear",))
        self.q["pool"].append(("go",))
        for e in ("pe", "act", "dve", "sp"):
            self.q[e].append(("wait", self.go, k))
        for e in ENGS:
            self.cnt[e] = 0
            self.hist[e] = {}
            self.seen[e] = {k2: v for k2, v in self.seen[e].items() if not isinstance(k2, str)}
        for r in self.res.values():
            if r.w is not None and r.w[0] == "e":
                r.w = None
            r.r = [d for d in r.r if d[0] != "e"]

    def replay(self, block):
        nc = self.nc
        q = self.q
        sem = self.sem

        def run(ename, eng):
            items = q[ename]
            fold = False
            pend = None
            for idx, it in enumerate(items):
                if it[0] == "wait":
                    if fold and idx + 1 < len(items) and items[idx + 1][0] == "op":
                        pend = (it[1], it[2])
                    else:
                        eng.wait_ge(it[1], it[2])
                elif it[0] == "op" and pend is not None:
                    ins = it[1](eng)
                    ins._wait_ge(pend[0], pend[1])
                    pend = None
                    if it[2]:
                        ins.then_inc(sem[ename], 1)
                elif it[0] == "raw":
                    it[1](eng)
                elif it[0] == "arrive":
                    eng.nop(nofuse=True).then_inc(self.arrive, 1)
                elif it[0] == "go":
                    eng.nop(nofuse=True).then_inc(self.go, 1)
                elif it[0] == "clear":
                    for e2 in ("pe", "act", "dve", "pool"):
                        eng.sem_clear(sem[e2])
                elif it[0] == "op":
                    ins = it[1](eng)
                    if it[2]:
                        ins.then_inc(sem[ename], 1)
                else:
                    eng.dma_start(out=it[1], in_=it[2]).then_inc(it[3], 16)

        @block.tensor
        def _(e):
            run("pe", e)

        @block.scalar
        def _(e):
            run("act", e)

        @block.vector
        def _(e):
            run("dve", e)

        @block.gpsimd
        def _(e):
            run("pool", e)

        @block.sync
        def _(e):
            run("sp", e)


class Unit:
    def __init__(self, kind, groups, layer=None, which=None, xn_reads=None):
        self.kind = kind
        self.groups = groups
        self.layer = layer
        self.which = which
        self.xn_reads = xn_reads if xn_reads is not None else list(groups)
        self.needs_norm = kind in ("ffn", "pool", "mla", "kvlat")


def make_units(cfg):
    part = cfg.get("part", 0)
    nl = cfg["n_layers"]
    layers = range(nl)
    if part == 1:
        layers = range(0, 2)
    elif part == 2:
        layers = range(2, 4)
    units = [Unit("loadh", ["A", "B"])] if part == 2 else [Unit("loadx", ["A", "B"])]
    for l in layers:
        if cfg["ffn"]:
            units += [Unit("ffn", ["A"], l, 1), Unit("ffn", ["B"], l, 1)]
        if cfg["mixer"]:
            if l < 2:
                if cfg.get("pool", True):
                    units += [Unit("pool", ["A"], l), Unit("pool", ["B"], l, xn_reads=["A", "B"])]
            if l >= 2 and cfg.get("mla", True):
                units += [Unit("mla", ["A", "B"], l)]
        if cfg["ffn"]:
            units += [Unit("ffn", ["A"], l, 2), Unit("ffn", ["B"], l, 2)]
        if l == 1 and (nl > 2 or part == 1) and cfg["mixer"]:
            units += [Unit("kvlat", ["A"], l), Unit("kvlat", ["B"], l)]
            if cfg.get("gather", True):
                units += [Unit("gather", [], l)]
    if part == 1:
        units += [Unit("storeh", [])]
    else:
        units += [Unit("final", ["A"]), Unit("final", ["B"])]
    return units


def tiles_of(unit, g):
    halo = unit.kind in ("loadx",) or (unit.layer is not None and unit.layer < 2)
    if g == "A":
        return TILES_A_H if halo else TILES_A
    return TILES_B


def build_nc(cfg):
    nc = bass.Bass("TRN2", target_bir_lowering=False)
    dt = nc.dram_tensor

    def ext(name, shape, dtype=F32):
        return dt(name, list(shape), dtype, kind="ExternalInput").ap()

    xslab = ext("xslab", [NT, D])
    gains = ext("gains", [128, G_COLS])
    ident_d = ext("ident", [128, 128])
    ic_d = ext("ic", [128, 8 * HALO])
    tcs_d = ext("tcs", [32, 2 * NT], BF16)
    valid_d = ext("valid", [128, 21])
    vtile_d = ext("vtile", [128, 81], BF16)
    mask_d = ext("mask", [128, 4 * 512], BF16)
    wg = [ext("ffn1_w_gate", [4, D, DFF]), ext("ffn2_w_gate", [4, D, DFF])]
    wu = [ext("ffn1_w_up", [4, D, DFF]), ext("ffn2_w_up", [4, D, DFF])]
    wd = [ext("ffn1_w_down", [4, DFF, D]), ext("ffn2_w_down", [4, DFF, D])]
    pool_w = ext("pool_w", [2, 4, 256, 256])
    w_dkv = ext("w_dkv", [D, 288])
    w_uk = ext("w_uk", [256, 512])
    w_uv = ext("w_uv", [256, 512])
    w_dq = ext("w_dq", [2, D, 384])
    w_uq = ext("w_uq", [2, 384, 768])
    w_o = ext("w_o", [2, 512, D])
    part = cfg.get("part", 0)
    lshape = [[128, NT], [128, NT], [32, NT]]
    ashape = [[512, NT], [512, NT], [128, NT]]
    if part == 2:
        LIN = [dt("lat_in%d" % i, lshape[i], BF16, kind="ExternalInput") for i in range(3)]
        LALL = [dt("lat_all%d" % i, ashape[i], BF16, kind="ExternalInput") for i in range(3)]
        h_in = dt("h_io", [128, NC_ * MAIN], F32, kind="ExternalInput").ap()
    else:
        LIN = [dt("lat_in%d" % i, lshape[i], BF16) for i in range(3)]
        LALL = [dt("lat_all%d" % i, ashape[i], BF16) for i in range(3)]
    if part == 1:
        LINO = [dt("lat_in%d_o" % i, lshape[i], BF16, kind="ExternalOutput") for i in range(3)]
        LALLO = [dt("lat_all%d_o" % i, ashape[i], BF16, kind="ExternalOutput") for i in range(3)]
        h_out = dt("h_io", [128, NC_ * MAIN], F32, kind="ExternalOutput").ap()
    else:
        out_d = dt("out", [MAIN, D], F32, kind="ExternalOutput").ap()

    units = make_units(cfg)

    with contextlib.ExitStack() as es:
        P = Prog(nc, es)
        sb = lambda name, shape, dtype: es.enter_context(nc.sbuf_tensor(name, list(shape), dtype))
        H = sb("H", [128, NC_, NT], F32)
        SCR = sb("SCR", [128, 39872], BF16)
        RING = [sb("ring%d" % i, [128, 4096], BF16) for i in range(4)]
        SQ = sb("SQ", [128, 8, 512], BF16)
        T1 = sb("T1", [128, 512], F32)
        RSTD = sb("RSTD", [128, 512], F32)
        SG = [sb("SG%d" % i, [128, 512], BF16) for i in range(2)]
        EPSV = sb("EPSV", [128, 1], F32)
        GN = sb("GN", [128, G_COLS], F32)
        ONES = sb("ONES", [128, 128], BF16)
        ONES4 = sb("ONES4", [128, 128], BF16)
        ONES3 = sb("ONES3", [128, 128], BF16)
        ONEF = sb("ONEF", [128, 64], F32)
        IDENT = sb("IDENT", [128, 128], F32)
        TCS = sb("TCS", [32, 2 * NT], BF16)
        IC = sb("IC", [128, 8, HALO], F32)
        VALID = sb("VALID", [128, 21], F32)
        VTILE = sb("VTILE", [128, 81], BF16)
        MASK = sb("MASK", [128, 4, 512], BF16)
        PS = es.enter_context(nc.psum_tensor("PS", [128, 8 * 512], F32))
        cc_sems = [es.enter_context(nc.semaphore("cc_sem%d" % i)) for i in range(3)]

        def bank(i, w=512, p0=0, p1=128):
            return PS[p0:p1, i * 512:i * 512 + w]

        XN = SCR[:, 0:NC_ * NT].rearrange("p (c t) -> p c t", c=NC_)
        AB0 = NC_ * NT
        ACTB = SCR[:, AB0:AB0 + NJ * 1056].rearrange("p (j t) -> p j t", j=NJ)

        def f32view(off_el, n_f32):
            return SCR[:, off_el:off_el + 2 * n_f32].bitcast(F32)

        cslot = P.slot("const")
        consts = [(GN[:], gains[:, :], "GN"), (IDENT[:], ident_d[:, :], "IDENT"),
                  (IC[:].rearrange("p c t -> p (c t)"), ic_d[:, :], "IC"), (TCS[:], tcs_d[:, :], "TCS"),
                  (VALID[:], valid_d[:, :], "VALID"), (VTILE[:], vtile_d[:, :], "VTILE"),
                  (MASK[:].rearrange("p d q -> p (d q)"), mask_d[:, :], "MASK")]
        for o, i_, nm in consts:
            P.dma("sp", cslot, o, i_, writes=[nm])
        for o, i_, nm in consts:
            P.res[nm].w = ("d", cslot, cslot.count)
        for tl, val, nm in ((EPSV, EPS, "EPSV"), (ONES, 1.0 / 1024, "ONES"), (ONES4, 1.0 / 256, "ONES"),
                            (ONES3, 1.0 / 384, "ONES"), (ONEF, 1.0, "ONEF")):
            P.op("dve", (lambda tl, val: (lambda e: e.memset(tl[:], val)))(tl, val), writes=[nm])

        ring_slots = [P.slot("ring%d" % i) for i in range(4)]
        wplan = []

        def plan_weights():
            for u in units:
                if u.kind == "ffn":
                    l, f = u.layer, u.which - 1
                    for sa in range(11):
                        j0 = sa * 256
                        wplan.append((("A", id(u), sa), [
                            (lambda R: R[:, 0:2048].rearrange("p (k j) -> p k j", k=8),
                             wg[f][l].rearrange("(k p) j -> p k j", p=128)[:, :, j0:j0 + 256]),
                            (lambda R: R[:, 2048:4096].rearrange("p (k j) -> p k j", k=8),
                             wu[f][l].rearrange("(k p) j -> p k j", p=128)[:, :, j0:j0 + 256]),
                        ]))
                    for mp in range(4):
                        for kh in range(2):
                            wplan.append((("B", id(u), mp, kh), [
                                (lambda R: R[:, 0:11 * 256].rearrange("p (k j) -> p k j", k=11),
                                 wd[f][l][kh * 1408:(kh + 1) * 1408, :].rearrange(
                                     "(k p) m -> p k m", p=128)[:, :, mp * 256:(mp + 1) * 256]),
                            ]))
                elif u.kind == "pool" and u.groups == ["A"]:
                    wplan.append((("PW", u.layer), [
                        (lambda R: R[:, 0:2048].rearrange("p (g k d) -> p g k d", g=4, k=2),
                         pool_w[u.layer].rearrange("g (k p) d -> p g k d", p=128)),
                    ]))
                elif u.kind == "kvlat" and u.groups == ["A"]:
                    wplan.append((("DKV",), [
                        (lambda R: R[:, 0:8 * 288].rearrange("p (k j) -> p k j", k=8),
                         w_dkv.rearrange("(k p) j -> p k j", p=128)),
                        (lambda R: R[:, 2304:2304 + 256].rearrange("p (k j) -> p k j", k=8)[:, :, 0:16],
                         w_dkv.rearrange("(k p) j -> p k j", p=128)[:, :, 272:288]),
                        (lambda R: R[:, 2304:2304 + 256].rearrange("p (k j) -> p k j", k=8)[:, :, 16:32],
                         w_dkv.rearrange("(k p) j -> p k j", p=128)[:, :, 256:272]),
                    ]))
                elif u.kind == "mla":
                    j = u.layer - 2
                    wplan.append((("DQ", j), [
                        (lambda R: R[:, 0:8 * 384].rearrange("p (k j) -> p k j", k=8),
                         w_dq[j].rearrange("(k p) j -> p k j", p=128)),
                    ]))
                    uq = w_uq[j].rearrange("(k p) (h e) -> p k h e", p=128, e=96)
                    ent = [(lambda R: R[:, 0:3 * 768].rearrange("p (k j) -> p k j", k=3),
                            w_uq[j].rearrange("(k p) j -> p k j", p=128))]
                    for kq in range(3):
                        ent.append(((lambda kq: (lambda R: R[:, 2304:2304 + 768].rearrange(
                            "p (k h e) -> p k h e", k=3, h=8)[:, kq, :, 0:16]))(kq), uq[:, kq, :, 80:96]))
                        ent.append(((lambda kq: (lambda R: R[:, 2304:2304 + 768].rearrange(
                            "p (k h e) -> p k h e", k=3, h=8)[:, kq, :, 16:32]))(kq), uq[:, kq, :, 64:80]))
                    wplan.append((("UQ", j), ent))
                    wplan.append((("UKV", j), [
                        (lambda R: R[:, 0:1024].rearrange("p (k j) -> p k j", k=2),
                         w_uk.rearrange("(k p) j -> p k j", p=128)),
                        (lambda R: R[:, 1024:2048].rearrange("p (k j) -> p k j", k=2),
                         w_uv.rearrange("(k p) j -> p k j", p=128)),
                    ]))
                    wplan.append((("WO", j), [
                        (lambda R: R[:, 0:4096].rearrange("p (k j) -> p k j", k=4),
                         w_o[j].rearrange("(k p) j -> p k j", p=128)),
                    ]))

        plan_weights()
        wpos = {k: i for i, (k, _) in enumerate(wplan)}
        wstate = {"issued": 0}

        def prefetch(upto):
            upto = min(upto, len(wplan) - 1)
            while wstate["issued"] <= upto:
                i = wstate["issued"]
                si = i % 4
                for dstf, src in wplan[i][1]:
                    P.dma("pool", ring_slots[si], dstf(RING[si]), src, writes=[("ring", si)])
                wstate["issued"] += 1

        def wslot(key, ahead=3):
            i = wpos[key]
            prefetch(i + ahead)
            return i % 4, RING[i % 4]

        def ACT(out, in_, func, reads, writes, **kw):
            P.op("act", lambda e: e.activation(out=out, in_=in_, func=func, **kw), reads=reads, writes=writes)

        def TT(eng, out, in0, in1, op, reads, writes):
            P.op(eng, lambda e: e.tensor_tensor(out=out, in0=in0, in1=in1, op=op), reads=reads, writes=writes)

        def STT(out, in0, scalar, in1, op0, op1, reads, writes):
            P.op("dve", lambda e: e.scalar_tensor_tensor(out=out, in0=in0, scalar=scalar, in1=in1,
                                                         op0=op0, op1=op1), reads=reads, writes=writes)

        def TS(eng, out, in0, s1, s2, op0, op1, reads, writes):
            if op1 is None:
                P.op(eng, lambda e: e.tensor_scalar(out=out, in0=in0, scalar1=s1, scalar2=None, op0=op0),
                     reads=reads, writes=writes)
            else:
                P.op(eng, lambda e: e.tensor_scalar(out=out, in0=in0, scalar1=s1, scalar2=s2, op0=op0, op1=op1),
                     reads=reads, writes=writes)

        def COPY(eng, out, in_, reads, writes):
            P.op(eng, lambda e: e.tensor_copy(out=out, in_=in_), reads=reads, writes=writes)

        def MEMSET(eng, out, val, writes):
            P.op(eng, lambda e: e.memset(out, val), writes=writes)

        def TR(out, in_, ident, reads, writes, signal):
            P.op("pe", lambda e: e.transpose(out, in_, ident), reads=reads, writes=writes, signal=signal)
        def mm(out, lhsT, rhs, start, stop, reads, writes, signal=None):
            P.op("pe", lambda e: e.matmul(out, lhsT, rhs, start=start, stop=stop),
                 reads=reads, writes=writes, signal=stop if signal is None else signal)

        sgrot = {"i": 0}

        def RSQ(w, ps_bank):
            ACT(T1[:, 0:w], bank(ps_bank, w), AF.Sqrt, [("ps", ps_bank), "EPSV"], ["T1"], bias=EPSV[:, 0:1])
            P.op("dve", (lambda o_, i_: (lambda e: e.reciprocal(o_, i_)))(RSTD[:, 0:w], T1[:, 0:w]),
                 reads=["T1"], writes=["RSTD"])

        def norm_stats(src_fn, nchunks, ones, w, ps_bank, src_reads):
            for half in range(0, nchunks, 4):
                n = min(4, nchunks - half)
                ACT(SQ[:, 0:n, 0:w], src_fn(half, n), AF.Square, src_reads, [("SQ", 0)])
                for c in range(n):
                    mm(bank(ps_bank, w), ones[:, :], SQ[:, c, 0:w], start=(half + c == 0),
                       stop=(half + c == nchunks - 1), reads=[("SQ", 0), "ONES"], writes=[("ps", ps_bank)],
                       signal=(c == n - 1))
            RSQ(w, ps_bank)

        def norm_steps(u, g):
            if u.kind == "ffn":
                gcol = (G_FFN1 if u.which == 1 else G_FFN2) + 8 * u.layer
            elif u.kind in ("pool", "mla"):
                gcol = G_MIX + 8 * u.layer
            else:
                gcol = G_KVIN
            tiles = tiles_of(u, g)

            def squares(s0, w):
                t = TID[s0]
                ACT(SQ[:, 0:4, 0:w], H[:, 0:4, s0:s0 + w], AF.Square, [("H", t)], [("SQ", 0)])
                ACT(SQ[:, 4:8, 0:w], H[:, 4:8, s0:s0 + w], AF.Square, [("H", t)], [("SQ", 1)])

            def rest(s0, w):
                t = TID[s0]
                for c in range(8):
                    mm(bank(6, w), ONES[:, :], SQ[:, c, 0:w], c == 0, c == 7, [("SQ", c // 4), "ONES"],
                       [("ps", 6)], signal=(c in (3, 7)))
                RSQ(w, 6)
                for c in range(8):
                    STT(XN[:, c, s0:s0 + w], H[:, c, s0:s0 + w], GN[:, gcol + c:gcol + c + 1],
                        RSTD[:, 0:w], ALU.mult, ALU.mult, [("H", t), "RSTD", "GN"], [("XN", t, c)])

            squares(*tiles[0])
            yield
            for idx, (s0, w) in enumerate(tiles):
                rest(s0, w)
                if idx + 1 < len(tiles):
                    squares(*tiles[idx + 1])
                yield

        XN_ALL = lambda t: [("XN", t, c) for c in range(8)]

        def emit_loadx(u):
            XS = [f32view(AB0 + i * 2048, 1024) for i in range(2)]
            xsl = [P.slot("xs%d" % i, scr=True) for i in range(2)]
            subs = [(0, 32)] + [(32 + 128 * i, 128) for i in range(16)]
            for n, (r0, rn) in enumerate(subs):
                b = n % 2
                P.dma("sp", xsl[b], XS[b][0:rn, :], xslab[r0:r0 + rn, :], writes=[("XS", b)])
                tkey = ("H", 0 if r0 == 0 else TID[32 + 512 * ((r0 - 32) // 512)])
                for half in range(2):
                    pb = 6 + half
                    for c in range(4):
                        cc = half * 4 + c
                        TR(PS[:, pb * 512 + c * 128:pb * 512 + c * 128 + rn],
                           XS[b][0:rn, cc * 128:(cc + 1) * 128], IDENT[0:rn, 0:rn],
                           [("XS", b), "IDENT"], [("ps", pb)], c == 3)
                    ACT(H[:, half * 4:half * 4 + 4, r0:r0 + rn],
                        PS[:, pb * 512:(pb + 1) * 512].rearrange("p (c t) -> p c t", c=4)[:, :, 0:rn],
                        AF.Copy, [("ps", pb)], [tkey])
            P.barrier()

        def emit_ffn(u, hook):
            g = u.groups[0]
            tiles = tiles_of(u, g)
            g0 = tiles[0][0]
            rot = 0
            for sa in range(11):
                si, R = wslot(("A", id(u), sa))
                WG = R[:, 0:2048].rearrange("p (k j) -> p k j", k=8)
                WU = R[:, 2048:4096].rearrange("p (k j) -> p k j", k=8)
                for jj in range(2):
                    j = 2 * sa + jj
                    for (s0, w) in tiles:
                        t = TID[s0]
                        bg, bu = 2 * (rot % 3), 2 * (rot % 3) + 1
                        rot += 1
                        for k in range(8):
                            mm(bank(bu, w), WU[:, k, jj * 128:(jj + 1) * 128], XN[:, k, s0:s0 + w],
                               k == 0, k == 7, [("ring", si), ("XN", t, k)], [("ps", bu)])
                        for k in range(8):
                            mm(bank(bg, w), WG[:, k, jj * 128:(jj + 1) * 128], XN[:, k, s0:s0 + w],
                               k == 0, k == 7, [("ring", si), ("XN", t, k)], [("ps", bg)])
                        sgi = sgrot["i"] % 2
                        sgrot["i"] += 1
                        ACT(SG[sgi][:, 0:w], bank(bg, w), AF.Silu, [("ps", bg)], [("SG", sgi)])
                        TT("dve", ACTB[:, j, s0 - g0:s0 - g0 + w], bank(bu, w), SG[sgi][:, 0:w], ALU.mult,
                           [("ps", bu), ("SG", sgi)], [("ACTB", j, t)])
                hook()
            for mp in range(4):
                si0, R0 = wslot(("B", id(u), mp, 0), 3)
                si1, R1 = wslot(("B", id(u), mp, 1), 2)
                WDh = [R0[:, 0:2816].rearrange("p (k j) -> p k j", k=11),
                       R1[:, 0:2816].rearrange("p (k j) -> p k j", k=11)]
                sis = [si0, si1]
                for (s0, w) in tiles:
                    t = TID[s0]
                    for mo in range(2):
                        m = 2 * mp + mo
                        bd = 4 + (rot % 2)
                        rot += 1
                        for k in range(NJ):
                            mm(bank(bd, w), WDh[k // 11][:, k % 11, mo * 128:(mo + 1) * 128],
                               ACTB[:, k, s0 - g0:s0 - g0 + w], k == 0, k == NJ - 1,
                               [("ring", sis[k // 11]), ("ACTB", k, t)], [("ps", bd)])
                        STT(H[:, m, s0:s0 + w], bank(bd, w), 0.5, H[:, m, s0:s0 + w], ALU.mult, ALU.add,
                            [("ps", bd), ("H", t)], [("H", t)])
                hook()

        def emit_pool(u, hook):
            P.barrier()
            g = u.groups[0]
            l = u.layer
            si, R = wslot(("PW", l))
            PW = R[:, 0:2048].rearrange("p (g k d) -> p g k d", g=4, k=2)
            P0 = f32view(AB0, 2 * 544).rearrange("p (c t) -> p c t", c=2)
            P1 = f32view(AB0 + 2176, 2 * 544).rearrange("p (c t) -> p c t", c=2)
            POOLED = SCR[:, AB0 + 4352:AB0 + 4352 + 8 * 512].rearrange("p (c t) -> p c t", c=8)
            for (s0, w) in tiles_of(u, g):
                t = TID[s0]
                for gi, W in enumerate(POOL_W):
                    c0 = 2 * gi
                    eng = "dve"
                    xr = [("XN", t, c0), ("XN", t, c0 + 1)]
                    if s0 == 0:
                        MEMSET(eng, P0[:, :, 0:15], 0.0, ["P0"])
                        COPY(eng, P0[:, :, 15:15 + w], XN[:, c0:c0 + 2, 0:w], xr, ["P0"])
                    else:
                        COPY(eng, P0[:, :, 0:15 + w], XN[:, c0:c0 + 2, s0 - 15:s0 + w],
                             xr + [("XN", t - 1, c0), ("XN", t - 1, c0 + 1)], ["P0"])
                    src, dst, sname, dname = P0, P1, "P0", "P1"
                    sh = 1
                    lo = 0
                    while sh < W:
                        nlo = lo + sh
                        TT(eng, dst[:, :, nlo:15 + w], src[:, :, nlo:15 + w], src[:, :, nlo - sh:15 + w - sh],
                           ALU.add, [sname], [dname])
                        src, dst, sname, dname = dst, src, dname, sname
                        lo = nlo
                        sh *= 2
                    if s0 == 0:
                        TT(eng, dst[:, :, 15:15 + w], src[:, :, 15:15 + w], IC[:, c0:c0 + 2, 0:w], ALU.mult,
                           [sname, "IC"], [dname])
                        TT(eng, POOLED[:, c0:c0 + 2, 0:w], dst[:, :, 15:15 + w], XN[:, c0:c0 + 2, 0:w],
                           ALU.subtract, [dname] + xr, [("POOLED", gi)])
                    else:
                        STT(POOLED[:, c0:c0 + 2, 0:w], src[:, :, 15:15 + w], 1.0 / W,
                            XN[:, c0:c0 + 2, s0:s0 + w], ALU.mult, ALU.subtract, [sname] + xr, [("POOLED", gi)])
                    for mc in range(2):
                        dch = c0 + mc
                        pb = 6 + (dch % 2)
                        for k in range(2):
                            mm(bank(pb, w), PW[:, gi, k, mc * 128:(mc + 1) * 128], POOLED[:, c0 + k, 0:w],
                               k == 0, k == 1, [("ring", si), ("POOLED", gi)], [("ps", pb)])
                        STT(H[:, dch, s0:s0 + w], bank(pb, w),
                            GN[:, G_PSC + 8 * l + dch:G_PSC + 8 * l + dch + 1], H[:, dch, s0:s0 + w],
                            ALU.mult, ALU.add, [("ps", pb), ("H", t), "GN"], [("H", t)])
            P.barrier()

        def emit_final(u):
            P.barrier()
            g = u.groups[0]
            YN = f32view(AB0, 8 * 512).rearrange("p (c t) -> p c t", c=8)
            OS = [f32view(AB0 + 8192 + i * 2048, 1024) for i in range(2)]
            osl = [P.slot("os%s%d" % (g, i), scr=True) for i in range(2)]
            n = 0
            for (s0, w) in tiles_of(u, g):
                t = TID[s0]
                norm_stats(lambda half, n_: H[:, half:half + n_, s0:s0 + w], 8, ONES, w, 6, [("H", t)])
                for c in range(8):
                    STT(YN[:, c, 0:w], H[:, c, s0:s0 + w], GN[:, G_FIN + c:G_FIN + c + 1], RSTD[:, 0:w],
                        ALU.mult, ALU.mult, [("H", t), "RSTD", "GN"], [("YN", c)])
                for sub in range(w // 128):
                    b = n % 2
                    n += 1
                    for half in range(2):
                        pb = 4 + half
                        for c in range(4):
                            cc = half * 4 + c
                            TR(PS[:, pb * 512 + c * 128:pb * 512 + (c + 1) * 128],
                               YN[:, cc, sub * 128:(sub + 1) * 128], IDENT[:, :],
                               [("YN", cc), "IDENT"], [("ps", pb)], c == 3)
                        if half == 0:
                            ACT(OS[b][:, 0:512], bank(pb), AF.Copy, [("ps", pb)], [("OS", b, 0)])
                        else:
                            COPY("dve", OS[b][:, 512:1024], bank(pb), [("ps", pb)], [("OS", b, 1)])
                    r0 = s0 - HALO + sub * 128
                    P.dma("sp", osl[b], out_d[r0:r0 + 128, :], OS[b][:, :], reads=[("OS", b, 0), ("OS", b, 1)])
            P.barrier()
            return osl

        def emit_kvlat(u, hook):
            P.barrier()
            g = u.groups[0]
            si, R = wslot(("DKV",))
            WD = R[:, 0:2304].rearrange("p (k j) -> p k j", k=8)
            WS = R[:, 2304:2560].rearrange("p (k j) -> p k j", k=8)
            SQK = SCR[:, AB0:AB0 + 1024].rearrange("p (c t) -> p c t", c=2)
            LATS = [SCR[:, AB0 + 1024 + i * 1024:AB0 + 2048 + i * 1024].rearrange("p (c t) -> p c t", c=2)
                    for i in range(2)]
            KR = [SCR[:, AB0 + 3072 + i * 512:AB0 + 3584 + i * 512] for i in range(2)]
            F1 = f32view(AB0 + 4096, 512)
            F2 = f32view(AB0 + 5120, 512)
            lsl = [P.slot("lat%s%d" % (g, i), scr=True) for i in range(2)]
            for n, (s0, w) in enumerate(tiles_of(u, g)):
                t = TID[s0]
                b = n % 2
                xr = [("XN", t, k) for k in range(8)]
                for mc in range(2):
                    for k in range(8):
                        mm(bank(mc, w), WD[:, k, mc * 128:(mc + 1) * 128], XN[:, k, s0:s0 + w], k == 0, k == 7,
                           [("ring", si), ("XN", t, k)], [("ps", mc)])
                for k in range(8):
                    mm(bank(2, w, 0, 32), WD[:, k, 256:288], XN[:, k, s0:s0 + w], k == 0, k == 7,
                       [("ring", si), ("XN", t, k)], [("ps", 2)])
                for k in range(8):
                    mm(bank(3, w, 0, 32), WS[:, k, :], XN[:, k, s0:s0 + w], k == 0, k == 7,
                       [("ring", si), ("XN", t, k)], [("ps", 3)])
                for mc in range(2):
                    ACT(SQK[:, mc, 0:w], bank(mc, w), AF.Square, [("ps", mc)], [("SQK", mc)])
                for mc in range(2):
                    mm(bank(6, w), ONES4[:, :], SQK[:, mc, 0:w], mc == 0, mc == 1, [("SQK", mc), "ONES"],
                       [("ps", 6)])
                RSQ(w, 6)
                for mc in range(2):
                    STT(LATS[b][:, mc, 0:w], bank(mc, w), GN[:, G_KVL + mc:G_KVL + mc + 1], RSTD[:, 0:w],
                        ALU.mult, ALU.mult, [("ps", mc), "RSTD", "GN"], [("LATS", b)])
                TT("dve", F1[0:32, 0:w], bank(2, w, 0, 32), TCS[0:32, s0:s0 + w], ALU.mult,
                   [("ps", 2), "TCS"], ["F1"])
                TT("dve", F2[0:32, 0:w], bank(3, w, 0, 32), TCS[0:32, NT + s0:NT + s0 + w], ALU.mult,
                   [("ps", 3), "TCS"], ["F2"])
                TT("dve", KR[b][0:32, 0:w], F1[0:32, 0:w], F2[0:32, 0:w], ALU.add, ["F1", "F2"], [("KR", b)])
                for mc in range(2):
                    P.dma("sp", lsl[b], LIN[mc][:, s0:s0 + w], LATS[b][:, mc, 0:w], reads=[("LATS", b)])
                P.dma("sp", lsl[b], LIN[2][:, s0:s0 + w], KR[b][0:32, 0:w], reads=[("KR", b)])
            P.barrier()

        def emit_gather(u):
            P.barrier()

            def mk(i):
                def fn(e):
                    e.collective_compute("AllGather", ALU.bypass, replica_groups=[[0, 1, 2, 3], [4, 5, 6, 7]],
                                         ins=[LIN[i].ap().opt()], outs=[LALL[i].ap().opt()]).then_inc(cc_sems[i])
                return fn
            for i in range(3):
                P.q["pool"].append(("raw", mk(i)))
                P.res.setdefault(("LATALL", i), Res()).w = ("s", cc_sems[i], 1)

        QT = [(32, 512), (544, 512), (1056, 512), (1568, 512)]

        def emit_mla(u):
            j = u.layer - 2
            P.barrier()
            o = AB0
            CQ = SCR[:, o:o + 6144].rearrange("p (c t) -> p c t", c=3); o += 6144
            OT = SCR[:, o:o + 8192].rearrange("p (c t) -> p c t", c=4); o += 8192
            QH = SCR[:, o:o + 2048]; o += 2048
            PT = [SCR[:, o + i * 1024:o + (i + 1) * 1024] for i in range(2)]; o += 2048
            NCL = 3
            CL = [SCR[:, o + i * 1024:o + (i + 1) * 1024].rearrange("p (c t) -> p c t", c=2) for i in range(2)]
            o += 2048
            CL.append(SCR[:, 10256 + 81 * 65:10256 + 81 * 65 + 1024].rearrange("p (c t) -> p c t", c=2))
            assert 10256 + 81 * 65 + 1024 <= AB0
            F1 = f32view(o, 512); o += 1024
            F2 = f32view(o, 512); o += 1024
            assert o <= AB0 + NJ * 1056
            KH = SCR[:, 0:10256]
            VA = SCR[:, 10256:10256 + 81 * 65].rearrange("p (t d) -> p t d", d=65)
            VAF = SCR[:, 10256:10256 + 81 * 65 + 64]

            def vblk(vt):
                return 0 if vt == 0 else (1 + (vt - 1) // 4 if vt <= 64 else 17 + (vt - 65) // 4)

            def va_reads(vt):
                r = [("VA", vblk(vt)), "VAONE"]
                if vt + 1 <= 80:
                    if vblk(vt + 1) != vblk(vt):
                        r.append(("VA", vblk(vt + 1)))
                else:
                    r.append(("CL", 2))
                return r
            assert 10256 + 81 * 65 <= AB0
            si_dq, RDQ = wslot(("DQ", j))
            WDQ = RDQ[:, 0:3072].rearrange("p (k j) -> p k j", k=8)
            for (s0, w) in QT:
                t = TID[s0]
                m0 = s0 - HALO
                for mc in range(3):
                    for k in range(8):
                        mm(bank(mc), WDQ[:, k, mc * 128:(mc + 1) * 128], XN[:, k, s0:s0 + 512], k == 0, k == 7,
                           [("ring", si_dq), ("XN", t, k)], [("ps", mc)])
                ACT(SQ[:, 0:3, :], PS[:, 0:1536].rearrange("p (c t) -> p c t", c=3), AF.Square,
                    [("ps", 0), ("ps", 1), ("ps", 2)], [("SQ", 0)])
                for mc in range(3):
                    mm(bank(6), ONES3[:, :], SQ[:, mc, :], mc == 0, mc == 2, [("SQ", 0), "ONES"], [("ps", 6)])
                RSQ(512, 6)
                for mc in range(3):
                    STT(CQ[:, mc, m0:m0 + 512], bank(mc), GN[:, G_QL + 3 * j + mc:G_QL + 3 * j + mc + 1],
                        RSTD[:, :], ALU.mult, ALU.mult, [("ps", mc), "RSTD", "GN"], [("CQ", t)])
            P.barrier()
            si_uq, RUQ = wslot(("UQ", j), 3)
            WUQ = RUQ[:, 0:2304].rearrange("p (k j) -> p k j", k=3)
            WUQS = RUQ[:, 2304:3072].rearrange("p (k h e) -> p k h e", k=3, h=8)
            si_kv, RKV = wslot(("UKV", j), 2)
            WUK = RKV[:, 0:1024].rearrange("p (k j) -> p k j", k=2)
            WUV = RKV[:, 1024:2048].rearrange("p (k j) -> p k j", k=2)
            si_o, RO = wslot(("WO", j), 1)
            WO = RO[:, 0:4096].rearrange("p (k j) -> p k j", k=4)
            COPY("dve", VA[:, :, 64], VTILE[:, :], ["VTILE"], ["VAONE"])
            clsl = [P.slot("cl%d_%d" % (j, i), scr=True) for i in range(NCL)]
            khsl = P.slot("khr%d" % j, scr=True)
            nblk = 0
            for h in range(NH):
                kh_keys = []
                for blk in range(21):
                    rd = [("LATALL", 0), ("LATALL", 1), ("LATALL", 2)]
                    if blk == 0:
                        srcs = [LALL[0][0:128, 16:32], LALL[1][0:128, 16:32], LALL[2][0:32, 16:32]]
                        nk, kc0, vt0 = 16, 0, 0
                    elif blk <= 16:
                        r, q4 = (blk - 1) // 4, (blk - 1) % 4
                        c0_, c1_ = HALO + 512 * q4, HALO + 512 * q4 + 512
                        srcs = [LALL[0][128 * r:128 * r + 128, c0_:c1_], LALL[1][128 * r:128 * r + 128, c0_:c1_],
                                LALL[2][32 * r:32 * r + 32, c0_:c1_]]
                        nk, kc0, vt0 = 512, 16 + 512 * (blk - 1), 1 + 4 * (blk - 1)
                    else:
                        q4 = blk - 17
                        c0_, c1_ = HALO + 512 * q4, HALO + 512 * q4 + 512
                        srcs = [LIN[0][:, c0_:c1_], LIN[1][:, c0_:c1_], LIN[2][:, c0_:c1_]]
                        nk, kc0, vt0 = 512, 16 + 8192 + 512 * q4, 65 + 4 * q4
                    b = nblk % NCL
                    nblk += 1
                    for c in range(2):
                        P.dma("sp" if c == 0 else "act", clsl[b], CL[b][:, c, 0:nk], srcs[c], reads=rd,
                              writes=[("CL", b)])
                    if h == 0:
                        P.dma("sp", khsl, KH[64:96, kc0:kc0 + nk], srcs[2], reads=rd, writes=[("KHR", blk)])
                        kh_keys.append(("KHR", blk))
                    nb = blk % 2
                    vb = 2 + (blk % 2)
                    hp, ho = h // 2, (h % 2) * 64
                    for c in range(2):
                        mm(bank(nb, nk), WUK[:, c, hp * 128:(hp + 1) * 128], CL[b][:, c, 0:nk], c == 0, c == 1,
                           [("ring", si_kv), ("CL", b)], [("ps", nb)])
                    COPY("dve", KH[0:64, kc0:kc0 + nk], bank(nb, nk, ho, ho + 64), [("ps", nb)], [("KHN", blk)])
                    kk = min(128, nk)
                    nt = max(1, nk // 128)
                    for ti in range(nt):
                        for c in range(2):
                            mm(PS[0:kk, vb * 512 + ti * 64:vb * 512 + ti * 64 + 64],
                               CL[b][:, c, ti * 128:ti * 128 + kk], WUV[:, c, h * 64:(h + 1) * 64], c == 0, c == 1,
                               [("ring", si_kv), ("CL", b)], [("ps", vb)], signal=(c == 1 and ti == nt - 1))
                    TS("dve", VA[0:kk, vt0:vt0 + nt, 0:64],
                       PS[0:kk, vb * 512:vb * 512 + nt * 64].rearrange("p (t d) -> p t d", d=64),
                       VALID[0:kk, blk:blk + 1], None, ALU.mult, None, [("ps", vb), "VALID"], [("VA", blk)])
                for kkey in kh_keys:
                    P.res[kkey].w = ("d", khsl, khsl.count)
                for (s0, w) in QT:
                    t = TID[s0]
                    m0 = s0 - HALO
                    for c in range(3):
                        mm(bank(5, 512, 0, 64), WUQ[:, c, h * 96:h * 96 + 64], CQ[:, c, m0:m0 + 512], c == 0, c == 2,
                           [("ring", si_uq), ("CQ", t)], [("ps", 5)])
                    for c in range(3):
                        mm(bank(6, 512, 0, 32), WUQ[:, c, h * 96 + 64:h * 96 + 96], CQ[:, c, m0:m0 + 512], c == 0,
                           c == 2, [("ring", si_uq), ("CQ", t)], [("ps", 6)])
                    for c in range(3):
                        mm(bank(7, 512, 0, 32), WUQS[:, c, h, :], CQ[:, c, m0:m0 + 512], c == 0, c == 2,
                           [("ring", si_uq), ("CQ", t)], [("ps", 7)])
                    ACT(QH[0:64, m0:m0 + 512], bank(5, 512, 0, 64), AF.Copy, [("ps", 5)], [("QH", t)])
                    TT("dve", F1[0:32, :], bank(6, 512, 0, 32), TCS[0:32, s0:s0 + 512], ALU.mult,
                       [("ps", 6), "TCS"], ["F1"])
                    TT("dve", F2[0:32, :], bank(7, 512, 0, 32), TCS[0:32, NT + s0:NT + s0 + 512], ALU.mult,
                       [("ps", 7), "TCS"], ["F2"])
                    TT("dve", QH[64:96, m0:m0 + 512], F1[0:32, :], F2[0:32, :], ALU.add, ["F1", "F2"],
                       [("QHR", t)])
                pend_norm = []
                for qi, (s0, w) in enumerate(QT):
                    t = TID[s0]
                    m0 = s0 - HALO
                    ktl = [(0, 16, 0, None, 0)]
                    for kt in range(64):
                        ktl.append((16 + 128 * kt, 128, 1 + kt, None, 1 + kt // 4))
                    for kt in range(4 * (qi + 1)):
                        dd = kt - 4 * qi
                        ktl.append((16 + 8192 + 128 * kt, 128, 65 + kt, dd if dd >= 0 else None, 17 + kt // 4))
                    ob = 4 + (qi % 2)
                    psO = PS[0:128, ob * 512:(ob + 1) * 512]
                    nkt = len(ktl)

                    def scores_pair(p, slot):
                        for half in range(2):
                            kc, nk, vt, dd, blk = ktl[1 + 2 * p + half]
                            sbk = 2 * slot + half
                            mm(bank(sbk, 512, 0, nk), KH[0:96, kc:kc + nk], QH[0:96, m0:m0 + 512], True, True,
                               [("KHR", blk), ("KHN", blk), ("QH", t), ("QHR", t)], [("ps", sbk)])

                    def expv_pair(p, slot):
                        pt = PT[slot]
                        ACT(pt[:, 0:1024], PS[:, 2 * slot * 512:2 * slot * 512 + 1024], AF.Exp,
                            [("ps", 2 * slot), ("ps", 2 * slot + 1)], [("PT", slot)], scale=SM_SCALE)
                        for half in range(2):
                            kc, nk, vt, dd, blk = ktl[1 + 2 * p + half]
                            if dd is not None:
                                TT("dve", pt[:, half * 512:(half + 1) * 512], pt[:, half * 512:(half + 1) * 512],
                                   MASK[:, dd, :], ALU.mult, [("PT", slot), "MASK"], [("PT", slot)])
                        for half in range(2):
                            kc, nk, vt, dd, blk = ktl[1 + 2 * p + half]
                            last = (1 + 2 * p + half == nkt - 1)
                            mm(psO, VAF[0:nk, vt * 65:vt * 65 + 128], pt[0:nk, half * 512:(half + 1) * 512], False,
                               last, va_reads(vt) + [("PT", slot)], [("ps", ob)], signal=True)

                    def meta_scores():
                        kc, nk, vt, dd, blk = ktl[0]
                        mm(bank(2, 512, 0, nk), KH[0:96, kc:kc + nk], QH[0:96, m0:m0 + 512], True, True,
                           [("KHR", blk), ("KHN", blk), ("QH", t), ("QHR", t)], [("ps", 2)])

                    def meta_expv():
                        kc, nk, vt, dd, blk = ktl[0]
                        pt = PT[1]
                        ACT(pt[0:nk, 0:512], bank(2, 512, 0, nk), AF.Exp, [("ps", 2)], [("PT", 1)], scale=SM_SCALE)
                        mm(psO, VAF[0:nk, vt * 65:vt * 65 + 128], pt[0:nk, 0:512], True, False,
                           va_reads(vt) + [("PT", 1)], [("ps", ob)], signal=True)

                    def normalise(ob=ob, m0=m0, t=t, h=h):
                        P.op("dve", (lambda o_, i_: (lambda e: e.reciprocal(o_, i_)))(
                            F2[64:65, :], PS[64:65, ob * 512:(ob + 1) * 512]), reads=[("ps", ob)], writes=["F2"])
                        mm(bank(6, 512, 0, 64), ONEF[64:65, 0:64], F2[64:65, :], True, True, ["F2", "ONEF"],
                           [("ps", 6)])
                        ACT(F1[0:64, :], PS[0:64, ob * 512:(ob + 1) * 512], AF.Copy, [("ps", ob)], ["F1"])
                        p0 = (h % 2) * 64
                        TT("dve", OT[p0:p0 + 64, h // 2, m0:m0 + 512], F1[0:64, :], bank(6, 512, 0, 64), ALU.mult,
                           ["F1", ("ps", 6)], [("OT", h // 2, t)])

                    npair = (nkt - 1) // 2
                    assert 1 + 2 * npair == nkt
                    meta_scores()
                    scores_pair(0, 0)
                    meta_expv()
                    for p in range(1, npair):
                        scores_pair(p, p % 2)
                        expv_pair(p - 1, (p - 1) % 2)
                        if p == 3 and pend_norm:
                            pend_norm.pop()()
                    expv_pair(npair - 1, (npair - 1) % 2)
                    if pend_norm:
                        pend_norm.pop()()
                    pend_norm.append(normalise)
                if pend_norm:
                    pend_norm.pop()()
            rot = 0
            for (s0, w) in QT:
                t = TID[s0]
                m0 = s0 - HALO
                for m in range(8):
                    pb = rot % 3
                    rot += 1
                    for p_ in range(4):
                        mm(bank(pb), WO[:, p_, m * 128:(m + 1) * 128], OT[:, p_, m0:m0 + 512], p_ == 0, p_ == 3,
                           [("ring", si_o), ("OT", p_, t)], [("ps", pb)])
                    TT("dve", H[:, m, s0:s0 + 512], bank(pb), H[:, m, s0:s0 + 512], ALU.add,
                       [("ps", pb), ("H", t)], [("H", t)])
            P.barrier()

        def emit_storeh(u):
            P.barrier()
            sl = P.slot("storeh", scr=True)
            rd = [("LATALL", 0), ("LATALL", 1), ("LATALL", 2)] + [("H", t) for t in range(1, 5)]
            for c in range(8):
                P.dma("sp", sl, h_out[:, c * MAIN:(c + 1) * MAIN], H[:, c, HALO:NT], reads=rd)
            for i in range(3):
                P.dma("sp", sl, LINO[i][:, :], LIN[i][:, :], reads=rd)
                P.dma("sp", sl, LALLO[i][:, :], LALL[i][:, :], reads=rd)
            return [sl]

        def emit_loadh(u):
            sl = P.slot("loadh", scr=True)
            for c in range(8):
                P.dma("sp", sl, H[:, c, HALO:NT], h_in[:, c * MAIN:(c + 1) * MAIN],
                      writes=[("H", t) for t in range(1, 5)])
            P.barrier()

        normed = set()
        emitted = set()

        def last_h_writer(idx, g):
            for i in range(idx - 1, -1, -1):
                if g in units[i].groups and units[i].kind in ("loadx", "loadh", "ffn", "pool", "mla"):
                    return i
            return -1

        pending = {}

        def ensure_norm(idx, g):
            u = units[idx]
            if not u.needs_norm:
                return
            if (idx, g) not in normed:
                normed.add((idx, g))
                pending[(idx, g)] = norm_steps(u, g)
            gen = pending.pop((idx, g), None)
            if gen is not None:
                for _ in gen:
                    pass

        def advance():
            for key in list(pending.keys()):
                gen = pending[key]
                try:
                    next(gen)
                except StopIteration:
                    pending.pop(key, None)
                break

        def try_ahead(i):
            for vi in range(i + 1, min(i + 3, len(units))):
                v = units[vi]
                if not v.needs_norm:
                    continue
                for g in v.groups:
                    if (vi, g) in normed:
                        continue
                    if last_h_writer(vi, g) >= i:
                        continue
                    blocked = any(g in units[k].xn_reads and k not in emitted for k in range(0, vi))
                    if blocked:
                        continue
                    normed.add((vi, g))
                    pending[(vi, g)] = norm_steps(v, g)
            advance()

        final_slots = []
        prefetch(2)
        for i, u in enumerate(units):
            for g in u.groups:
                ensure_norm(i, g)
            hook = (lambda i=i: try_ahead(i))
            if u.kind == "loadx":
                emit_loadx(u)
            elif u.kind == "loadh":
                emit_loadh(u)
            elif u.kind == "storeh":
                final_slots += emit_storeh(u)
            elif u.kind == "ffn":
                emit_ffn(u, hook)
            elif u.kind == "pool":
                emit_pool(u, hook)
            elif u.kind == "kvlat":
                emit_kvlat(u, hook)
            elif u.kind == "gather":
                emit_gather(u)
                if cfg.get("part", 0) == 0:
                    P.hard_reset()
            elif u.kind == "mla":
                emit_mla(u)
            elif u.kind == "final":
                final_slots += emit_final(u)
            else:
                raise NotImplementedError(u.kind)
            emitted.add(i)

        for s in final_slots:
            if s.count:
                P._need("sp", ("d", s, s.count), "raw")
                P._need("pe", ("d", s, s.count), "raw")

        with nc.Block() as block:
            P.replay(block)
    return nc


def rope_tables(pos):
    inv = (1.0 / (10000.0 ** (np.arange(0, 32, 2, dtype=np.float32) / np.float32(32)))).astype(np.float32)
    ang = pos.astype(np.float32)[:, None] * inv[None, :]
    return np.cos(ang).astype(np.float32), np.sin(ang).astype(np.float32)


def host_inputs(inputs, n_cores=8):
    x = np.asarray(inputs["x"], dtype=np.float32)
    meta = np.asarray(inputs["meta_tokens"], dtype=np.float32)

    def cols(a):
        a = np.asarray(a, dtype=np.float32)
        a = a.reshape(a.shape[0], -1, 128)
        return np.ascontiguousarray(a.transpose(2, 0, 1).reshape(128, -1))

    gains = np.concatenate([
        cols(inputs["ffn1_norm"]), cols(inputs["mix_norm"]), cols(inputs["ffn2_norm"]),
        cols(inputs["pool_scale"]), cols(np.asarray(inputs["kv_in_norm"])[None]),
        cols(np.asarray(inputs["final_norm"])[None]), cols(np.asarray(inputs["kv_latent_norm"])[None]),
        cols(inputs["q_latent_norm"]),
    ], axis=1)
    assert gains.shape == (128, G_COLS)
    ident = np.eye(128, dtype=np.float32)
    k = np.arange(128)[:, None, None]
    d = np.arange(4)[None, :, None]
    q = np.arange(512)[None, None, :]
    mask = ((128 * d + k) // 64 <= q // 64).astype(np.float32).reshape(128, 2048).astype(ml_dtypes.bfloat16)
    shared = {
        "gains": gains, "ident": ident, "mask": mask,
    }
    for name in ("ffn1_w_gate", "ffn1_w_up", "ffn1_w_down", "ffn2_w_gate", "ffn2_w_up", "ffn2_w_down",
                 "pool_w", "w_dkv", "w_uk", "w_uv", "w_dq", "w_uq", "w_o"):
        shared[name] = np.ascontiguousarray(np.asarray(inputs[name], dtype=np.float32))
    maps = []
    for i in range(n_cores):
        b, c = i // 4, i % 4
        slab = np.zeros((NT, D), np.float32)
        if c == 0:
            slab[16:32] = meta
            slab[32:] = x[b, 0:MAIN]
        else:
            slab[:] = x[b, MAIN * c - HALO:MAIN * c + MAIN]
        pos = (MAIN * c - 16 + np.arange(NT)).astype(np.float32)
        cs, sn = rope_tables(np.maximum(pos, 0))
        tc = np.concatenate([cs.T, cs.T], axis=0)
        ts = np.concatenate([-sn.T, sn.T], axis=0)
        tcs = np.concatenate([tc, ts], axis=1).astype(ml_dtypes.bfloat16)
        ic = np.zeros((128, 8, HALO), np.float32)
        for ch in range(8):
            W = POOL_W[ch // 2]
            ic[:, ch, :] = 1.0 / W
            if c == 0:
                for s in range(16, 32):
                    ic[:, ch, s] = 1.0 / min(W, s - 16 + 1)
        valid = np.zeros((128, 21), np.float32)
        valid[:, 0] = 1.0
        for blk in range(1, 17):
            valid[:, blk] = 1.0 if (blk - 1) // 4 < c else 0.0
        valid[:, 17:] = 1.0
        vtile = np.zeros((128, 81), np.float32)
        vtile[:, 0] = 1.0
        for blk in range(1, 21):
            vtile[:, 1 + 4 * (blk - 1):1 + 4 * blk] = valid[:, blk:blk + 1]
        m = dict(shared)
        m.update({"xslab": slab, "tcs": tcs, "ic": ic.reshape(128, 8 * HALO), "valid": valid,
                  "vtile": vtile.astype(ml_dtypes.bfloat16)})
        maps.append(m)
    return maps


FULL_CFG = {"n_layers": 4, "ffn": True, "mixer": True}
_CACHE = {}


def get_nc(cfg):
    key = tuple(sorted(cfg.items()))
    if key not in _CACHE:
        _CACHE[key] = build_nc(cfg)
    return _CACHE[key]


def run(inputs, cfg):
    maps = host_inputs(inputs)
    if cfg.get("split", False):
        c1 = dict(cfg); c1.pop("split"); c1["part"] = 1
        c2 = dict(c1); c2["part"] = 2
        r1 = run_bass_kernel_spmd(get_nc(c1), maps, core_ids=list(range(8)))
        maps2 = []
        for i in range(8):
            m = dict(maps[i])
            o = r1.results[i]
            m["h_io"] = o["h_io"]
            for k in range(3):
                m["lat_in%d" % k] = o["lat_in%d_o" % k]
                m["lat_all%d" % k] = o["lat_all%d_o" % k]
            maps2.append(m)
        res = run_bass_kernel_spmd(get_nc(c2), maps2, core_ids=list(range(8)))
    else:
        res = run_bass_kernel_spmd(get_nc(cfg), maps, core_ids=list(range(8)))
    out = np.zeros((2, 8192, D), np.float32)
    for i in range(8):
        b, c = i // 4, i % 4
        out[b, MAIN * c:MAIN * (c + 1)] = res.results[i]["out"]
    return out


def kernel(**inputs):
    cfg = dict(FULL_CFG)
    return run(inputs, cfg)
```
